# Optimizing a Trainium2 kernel written in Bass

```python
import jax, jax.numpy as jnp
from jax import lax
import numpy as np

D_MODEL = 2048
BATCH = 2
SEQ = 16384
DEPTH = 2

CHUNK = 64
Q_BLOCK = 128
HEAD_DIM = 128
N_HEADS = 4
BRANCH_W = N_HEADS * HEAD_DIM
N_BRANCH = 3
LEFT_CHUNKS = 8
BAND = (LEFT_CHUNKS + 1) * CHUNK
MAX_REL = 128
N_REL = 2 * MAX_REL + 1
QKVG_COLS = N_BRANCH * 4 * BRANCH_W
IN_COLS = QKVG_COLS + N_HEADS + N_BRANCH * D_MODEL
EPS = 1e-6
NEG = -1e30

kernel_name = "hybrid_fox_chunkrel_stickbreak_gated"


def rmsnorm(x, g):
    xf = x.astype(jnp.float32)
    y = xf * lax.rsqrt(jnp.mean(xf * xf, axis=-1, keepdims=True) + EPS)
    return (y * g.astype(jnp.float32)).astype(x.dtype)


def to_heads(t):
    b, s, _ = t.shape
    return t.reshape(b, s, N_HEADS, HEAD_DIM).transpose(0, 2, 1, 3)


def from_heads(o):
    b, h, s, d = o.shape
    return o.transpose(0, 2, 1, 3).reshape(b, s, h * d)


def to_blocks(t, size):
    b, h, s, d = t.shape
    return t.reshape(b, h, s // size, size, d).transpose(2, 0, 1, 3, 4)


def from_blocks(o):
    nb, b, h, size, d = o.shape
    return o.transpose(1, 2, 0, 3, 4).reshape(b, h, nb * size, d)


def forgetting_attention(q, k, v, f_logit):
    b, h, s, d = q.shape
    nb = s // Q_BLOCK
    scale = d ** -0.5
    log_f = jax.nn.log_sigmoid(f_logit.astype(jnp.float32))
    c = jnp.cumsum(log_f, axis=1).transpose(0, 2, 1)
    outs = []
    for i in range(nb):
        lo, hi = i * Q_BLOCK, (i + 1) * Q_BLOCK
        logits = jnp.einsum('bhqd,bhkd->bhqk', q[:, :, lo:hi], k[:, :, :hi]).astype(jnp.float32) * scale
        logits = logits + c[:, :, lo:hi, None] - c[:, :, None, :hi]
        mask = jnp.arange(hi)[None, :] <= jnp.arange(lo, hi)[:, None]
        logits = jnp.where(mask, logits, NEG)
        p = jax.nn.softmax(logits, axis=-1).astype(v.dtype)
        outs.append(jnp.einsum('bhqk,bhkd->bhqd', p, v[:, :, :hi]))
    return jnp.concatenate(outs, axis=2)


def chunked_relpos_attention(q, k, v, rel_bias):
    b, h, s, d = q.shape
    nc = s // CHUNK
    scale = d ** -0.5
    pad = LEFT_CHUNKS * CHUNK
    kp = jnp.pad(k, ((0, 0), (0, 0), (pad, 0), (0, 0)))
    vp = jnp.pad(v, ((0, 0), (0, 0), (pad, 0), (0, 0)))
    qi_pos = jnp.arange(CHUNK)
    band_pos = jnp.arange(BAND)
    rel = pad + qi_pos[:, None] - band_pos[None, :]
    idx = jnp.clip(rel, -MAX_REL, MAX_REL) + MAX_REL
    bias = rel_bias[:, idx].astype(jnp.float32)

    def chunk(args):
        qc, ci = args
        start = ci * CHUNK
        kb = lax.dynamic_slice_in_dim(kp, start, BAND, axis=2)
        vb = lax.dynamic_slice_in_dim(vp, start, BAND, axis=2)
        logits = jnp.einsum('bhqd,bhkd->bhqk', qc, kb).astype(jnp.float32) * scale + bias
        valid = (start - pad + band_pos) >= 0
        logits = jnp.where(valid, logits, NEG)
        p = jax.nn.softmax(logits, axis=-1).astype(v.dtype)
        return jnp.einsum('bhqk,bhkd->bhqd', p, vb)

    o = lax.map(chunk, (to_blocks(q, CHUNK), jnp.arange(nc)))
    return from_blocks(o)


def stick_breaking_attention(q, k, v):
    b, h, s, d = q.shape
    nb = s // Q_BLOCK
    scale = d ** -0.5
    ar = jnp.arange(Q_BLOCK)
    upper = (ar[:, None] >= ar[None, :]).astype(jnp.float32)
    hp = lax.Precision.HIGHEST
    outs = []
    for i in range(nb):
        nk = i + 1
        lo, hi = i * Q_BLOCK, nk * Q_BLOCK
        z = jnp.einsum('bhqd,bhkd->bhqk', q[:, :, lo:hi], k[:, :, :hi]).astype(jnp.float32) * scale
        mask = jnp.arange(hi)[None, :] < jnp.arange(lo, hi)[:, None]
        lk = jnp.where(mask, jax.nn.log_sigmoid(-z), 0.0).reshape(b, h, Q_BLOCK, nk, Q_BLOCK)
        r_in = jnp.einsum('bhqnk,kj->bhqnj', lk, upper, precision=hp)
        tot = r_in[..., 0]
        nbk = jnp.arange(nk)
        later_m = (nbk[:, None] > nbk[None, :]).astype(jnp.float32)
        later = jnp.einsum('bhqm,mn->bhqn', tot, later_m, precision=hp)
        r = (r_in + later[..., None]).reshape(b, h, Q_BLOCK, hi)
        a = jnp.where(mask, jnp.exp(z + r), 0.0).astype(v.dtype)
        outs.append(jnp.einsum('bhqk,bhkd->bhqd', a, v[:, :, :hi]))
    return jnp.concatenate(outs, axis=2)


def hybrid_layer(x, norm_g, w_in, b_f, b_gate, qk_norm_g, rel_bias, w_up, w_out):
    b, s, _ = x.shape
    hn = rmsnorm(x, norm_g)
    proj = jnp.einsum('bsd,dc->bsc', hn, w_in)
    br = proj[..., :QKVG_COLS].reshape(b, s, N_BRANCH, 4, BRANCH_W)
    f_logit = proj[..., QKVG_COLS:QKVG_COLS + N_HEADS] + b_f
    gates = jax.nn.sigmoid(proj[..., QKVG_COLS + N_HEADS:] + b_gate).reshape(b, s, N_BRANCH, D_MODEL)

    qa = rmsnorm(to_heads(br[:, :, 0, 0]), qk_norm_g[0])
    ka = rmsnorm(to_heads(br[:, :, 0, 1]), qk_norm_g[1])
    ya = forgetting_attention(qa, ka, to_heads(br[:, :, 0, 2]), f_logit)
    qb = rmsnorm(to_heads(br[:, :, 1, 0]), qk_norm_g[2])
    kb = rmsnorm(to_heads(br[:, :, 1, 1]), qk_norm_g[3])
    yb = chunked_relpos_attention(qb, kb, to_heads(br[:, :, 1, 2]), rel_bias)
    yc = stick_breaking_attention(to_heads(br[:, :, 2, 0]), to_heads(br[:, :, 2, 1]),
                                  to_heads(br[:, :, 2, 2]))

    ys = jnp.stack([from_heads(ya), from_heads(yb), from_heads(yc)], axis=2)
    ys = ys * jax.nn.silu(br[:, :, :, 3])
    up = jnp.einsum('bsnw,nwd->bsnd', ys, w_up)
    merged = jnp.sum(gates * up, axis=2)
    return x + jnp.einsum('bsd,de->bse', merged, w_out)


def setup_inputs(seed: int = 0) -> dict:
    key = jax.random.key(seed)
    ks = jax.random.split(key, 10)
    f32 = jnp.float32
    x = jax.random.normal(ks[0], (BATCH, SEQ, D_MODEL), f32)
    norm_g = 1.0 + 0.02 * jax.random.normal(ks[1], (DEPTH, D_MODEL), f32)
    w_in = jax.random.normal(ks[2], (DEPTH, D_MODEL, IN_COLS), f32) * D_MODEL ** -0.5
    b_f = 3.0 + 0.1 * jax.random.normal(ks[3], (DEPTH, N_HEADS), f32)
    b_gate = 0.02 * jax.random.normal(ks[4], (DEPTH, N_BRANCH * D_MODEL), f32)
    qk_norm_g = 1.0 + 0.02 * jax.random.normal(ks[5], (DEPTH, 4, HEAD_DIM), f32)
    rel_bias = 0.5 * jax.random.normal(ks[6], (DEPTH, N_HEADS, N_REL), f32)
    w_up = jax.random.normal(ks[7], (DEPTH, N_BRANCH, BRANCH_W, D_MODEL), f32) * BRANCH_W ** -0.5
    w_out = jax.random.normal(ks[8], (DEPTH, D_MODEL, D_MODEL), f32) * D_MODEL ** -0.5
    return {"x": x, "norm_g": norm_g, "w_in": w_in, "b_f": b_f, "b_gate": b_gate,
            "qk_norm_g": qk_norm_g, "rel_bias": rel_bias, "w_up": w_up, "w_out": w_out}


def reference(x, norm_g, w_in, b_f, b_gate, qk_norm_g, rel_bias, w_up, w_out):
    for layer in range(DEPTH):
        x = hybrid_layer(x, norm_g[layer], w_in[layer], b_f[layer], b_gate[layer],
                         qk_norm_g[layer], rel_bias[layer], w_up[layer], w_out[layer])
    return x
```

```python
import numpy as np
from contextlib import ExitStack
import ml_dtypes
import concourse.bass as bass
import concourse.mybir as mybir
from concourse.bass_utils import run_bass_kernel_spmd

F32 = mybir.dt.float32
BF16 = mybir.dt.bfloat16
AF = mybir.ActivationFunctionType
ALU = mybir.AluOpType
AX = mybir.AxisListType

D = 2048
NCH = 16
HD = 128
EPS = 1e-6
SCALE = HD ** -0.5
QKVG = 6144
N_HEADS = 4
FM_COLS = 768
TM_COLS = 769
WC_COLS = FM_COLS + TM_COLS


class Buf:
    __slots__ = ("ap", "w", "r", "name")

    def __init__(self, ap, name=""):
        self.ap = ap
        self.w = None
        self.r = []
        self.name = name

    def __getitem__(self, idx):
        return self.ap[idx]


class DSem:
    def __init__(self, sem, name):
        self.sem = sem
        self.n = 0
        self.name = name


class Sch:
    def __init__(self, nc, es):
        self.nc = nc
        self.es = es
        self.eng = {"pe": nc.tensor, "act": nc.scalar, "dve": nc.vector, "pool": nc.gpsimd,
                    "sp": nc.sync}
        self.sem = {k: es.enter_context(nc.semaphore("prog_" + k)) for k in self.eng}
        self.cnt = {k: 0 for k in self.eng}
        self.seen = {k: {} for k in self.eng}
        self.nd = 0
        self.ninstr = 0

    def dsem(self, name=None):
        self.nd += 1
        name = name or f"dma{self.nd}"
        return DSem(self.es.enter_context(self.nc.semaphore(name)), name)

    def sb(self, name, shape, dt):
        return Buf(self.es.enter_context(self.nc.sbuf_tensor(name, shape, dt)), name)

    def ps(self, name, shape, dt):
        return Buf(self.es.enter_context(self.nc.psum_tensor(name, shape, dt)), name)

    def _wait(self, e, toks):
        seen = self.seen[e]
        for t in toks:
            if t is None:
                continue
            key, sem, val = t
            if key == "pe" and e == "pe":
                continue
            if seen.get(key, 0) >= val:
                continue
            self.eng[e].wait_ge(sem, val)
            seen[key] = val

    def _deps(self, reads, writes):
        deps = []
        for b in reads:
            deps.append(b.w)
        for b in writes:
            deps.append(b.w)
            deps.extend(b.r)
        return deps

    def _commit(self, tok, reads, writes):
        for b in writes:
            b.w = tok
            b.r = []
        for b in reads:
            if b not in writes:
                b.r.append(tok)
                if len(b.r) > 24:
                    b.r = b.r[-24:]

    def op(self, e, fn, reads=(), writes=(), extra=()):
        self._wait(e, self._deps(reads, writes) + list(extra))
        ins = fn(self.eng[e])
        self.cnt[e] += 1
        ins.then_inc(self.sem[e], 1)
        self.ninstr += 1
        tok = (e, self.sem[e], self.cnt[e])
        self._commit(tok, reads, writes)
        return tok

    def group(self, e, fns, reads=(), writes=(), extra=()):
        self._wait(e, self._deps(reads, writes) + list(extra))
        ins = None
        for fn in fns:
            ins = fn(self.eng[e])
            self.ninstr += 1
        self.cnt[e] += 1
        ins.then_inc(self.sem[e], 1)
        tok = (e, self.sem[e], self.cnt[e])
        self._commit(tok, reads, writes)
        return tok

    def dma(self, e, out, in_, ds, reads=(), writes=(), extra=()):
        self._wait(e, self._deps(reads, writes) + list(extra))
        ins = self.eng[e].dma_start(out=out, in_=in_)
        ds.n += 16
        ins.then_inc(ds.sem, 16)
        self.ninstr += 1
        tok = (ds.name, ds.sem, ds.n)
        self._commit(tok, reads, writes)
        return tok

    def finish(self, toks):
        self._wait("sp", toks)


def _consts(s):
    c = {}
    c["ident"] = s.sb("ident", [128, 128], BF16)
    s.op("pool", lambda e: e.memset(c["ident"][:], 1.0), writes=[c["ident"]])
    s.op("pool", lambda e: e.affine_select(out=c["ident"][:], in_=c["ident"][:], pattern=[[-1, 128]],
                                           compare_op=ALU.is_equal, fill=0.0, base=0,
                                           channel_multiplier=1),
         reads=[c["ident"]], writes=[c["ident"]])
    return c


def emit_proj(s, S, x, ng, wc, qkg, bfv, qT, vg, cpos_out, cref_out, C):
    nc = s.nc
    NT = S // 128
    NG = S // 512
    ident = C["ident"]
    W = s.sb("W", [128, NCH, WC_COLS], BF16)
    gB = s.sb("gB", [128, D], F32)
    gcol = s.sb("gcol", [128, 6], F32)
    nbf = s.sb("nbf", [128, 1], F32)
    ones_inv = s.sb("ones_inv", [128, 128], BF16)
    xt = [s.sb(f"xt{i}", [128, D], F32) for i in range(3)]
    xds = [s.dsem(f"xds{i}") for i in range(3)]
    junk = s.sb("junk", [128, D], BF16)
    ss = s.sb("ss", [128, 4], F32)
    hn = [s.sb(f"hn{i}", [128, D], BF16) for i in range(2)]
    hnT = [s.sb(f"hnT{i}", [128, NCH, 512], BF16) for i in range(2)]
    vgt = [s.sb(f"vgt{i}", [128, 768], BF16) for i in range(2)]
    vds = [s.dsem(f"vds{i}") for i in range(2)]
    ebuf = s.sb("ebuf", [128, 384], F32)
    ef = s.sb("ef", [128, 2], F32)
    lfcol = s.sb("lfcol", [128, 128], F32)
    sq = s.sb("sq", [128, 512], BF16)
    lnv = s.sb("lnv", [128, 512], F32)
    rstd = s.sb("rstdb", [128, 512], F32)
    qTt = [s.sb(f"qTt{i}", [128, 6, 512], BF16) for i in range(2)]
    qds = [s.dsem(f"qds{i}") for i in range(2)]
    psT = s.ps("psT", [128, D], BF16)
    psA = s.ps("psA", [128, 512], F32)
    psB = s.ps("psB", [128, 512], F32)
    psF = [s.ps(f"psF{i}", [128, 512], F32) for i in range(2)]
    psM = s.ps("psM", [128, 512], F32)
    ids = s.dsem("init")

    for c in range(NCH):
        s.dma("pool", W[:, c, :], wc[c * 128:(c + 1) * 128, :], ids, writes=[W])
    s.dma("sp", gB[:], ng.partition_broadcast(128), s.dsem("i_gB"), writes=[gB])
    s.dma("sp", gcol[:, 0:4], qkg, s.dsem("i_gcol"), writes=[gcol])
    s.dma("sp", nbf[:], bfv, s.dsem("i_nbf"), writes=[nbf])
    s.op("pool", lambda e: e.memset(ones_inv[:], 1.0 / 128.0), writes=[ones_inv])
    s.op("pool", lambda e: e.memset(lfcol[:], 0.0), writes=[lfcol])
    s.op("pool", lambda e: e.memset(gcol[:, 4:6], 1.0), reads=[], writes=[gcol])
    for j in (0, 2, 4):
        s.op("dve", lambda e, j=j: e.tensor_scalar(out=gcol[:, j:j + 1], in0=gcol[:, j:j + 1],
                                                    scalar1=SCALE, scalar2=None, op0=ALU.mult),
             reads=[gcol], writes=[gcol])
    s.op("dve", lambda e: e.tensor_scalar(out=nbf[:], in0=nbf[:], scalar1=-1.0, scalar2=None,
                                          op0=ALU.mult), reads=[nbf], writes=[nbf])

    out_toks = []
    import os
    STOP = int(os.environ.get("PROJ_STOP", "99"))
    if STOP <= 1:
        return [W.w, gB.w, gcol.w, nbf.w, ones_inv.w]
    for g in range(NG):
        hT = hnT[g % 2]
        for tt in range(4):
            ti = 4 * g + tt
            xs = xt[ti % 3]
            s.dma("sp", xs[:], x[ti * 128:(ti + 1) * 128, :], xds[ti % 3], writes=[xs])
            s.op("pool", lambda e: e.memset(ss[:, 0:1], 0.0), writes=[ss])
            s.op("act", lambda e: e.activation(out=junk[:], in_=xs[:], func=AF.Square,
                                               accum_out=ss[:, 0:1]),
                 reads=[xs], writes=[junk, ss])
            s.op("act", lambda e: e.activation(out=ss[:, 1:2], in_=ss[:, 0:1], func=AF.Ln,
                                               scale=1.0 / D, bias=EPS), reads=[ss], writes=[ss])
            s.op("act", lambda e: e.activation(out=ss[:, 2:3], in_=ss[:, 1:2], func=AF.Exp,
                                               scale=-0.5), reads=[ss], writes=[ss])
            h = hn[ti % 2]
            s.op("dve", lambda e: e.scalar_tensor_tensor(out=h[:], in0=xs[:], scalar=ss[:, 2:3],
                                                         in1=gB[:], op0=ALU.mult, op1=ALU.mult),
                 reads=[xs, ss, gB], writes=[h])
            s.group("pe", [lambda e, c=c: e.transpose(psT[:, c * 128:(c + 1) * 128],
                                                      h[:, c * 128:(c + 1) * 128], ident[:])
                           for c in range(NCH)], reads=[h, ident], writes=[psT])
            pv = psT.ap[:].rearrange("p (c t) -> p c t", c=NCH)
            s.op("act", lambda e: e.copy(out=hT[:, 0:8, tt * 128:(tt + 1) * 128], in_=pv[:, 0:8, :]),
                 reads=[psT], writes=[hT])
            s.op("dve", lambda e: e.tensor_copy(hT[:, 8:16, tt * 128:(tt + 1) * 128], pv[:, 8:16, :]),
                 reads=[psT], writes=[hT])
            if STOP <= 2:
                return [hT.w] + hT.r
            fns = []
            for c in range(NCH):
                fns.append(lambda e, c=c: e.matmul(psA[:], hT[:, c, tt * 128:(tt + 1) * 128],
                                                   W[:, c, 768:1280], start=(c == 0), stop=(c == NCH - 1)))
            for c in range(NCH):
                fns.append(lambda e, c=c: e.matmul(psB[:, 0:257], hT[:, c, tt * 128:(tt + 1) * 128],
                                                   W[:, c, 1280:1537], start=(c == 0), stop=(c == NCH - 1)))
            s.group("pe", fns, reads=[hT, W], writes=[psA, psB])
            vt = vgt[ti % 2]
            s.op("act", lambda e: e.copy(out=vt[:, 0:384], in_=psA[:, 0:384]), reads=[psA], writes=[vt])
            s.op("act", lambda e: e.activation(out=ebuf[:, 0:128], in_=psA[:, 384:512], func=AF.Exp,
                                               scale=-1.0), reads=[psA], writes=[ebuf])
            s.op("act", lambda e: e.activation(out=ebuf[:, 128:384], in_=psB[:, 0:256], func=AF.Exp,
                                               scale=-1.0), reads=[psB], writes=[ebuf])
            s.op("act", lambda e: e.activation(out=ef[:, 0:1], in_=psB[:, 256:257], func=AF.Exp,
                                               scale=-1.0, bias=nbf[:, 0:1]), reads=[psB, nbf], writes=[ef])
            s.op("act", lambda e: e.activation(out=lfcol[:, ti:ti + 1], in_=ef[:, 0:1], func=AF.Ln,
                                               bias=1.0), reads=[ef], writes=[lfcol])
            s.op("pool", lambda e: e.tensor_scalar(out=ebuf[:], in0=ebuf[:], scalar1=1.0, scalar2=None,
                                                   op0=ALU.add), reads=[ebuf], writes=[ebuf])
            s.op("dve", lambda e: e.reciprocal(out=ebuf[:], in_=ebuf[:]), reads=[ebuf], writes=[ebuf])
            s.op("dve", lambda e: e.tensor_tensor(out=vt[:, 384:512], in0=psA[:, 384:512],
                                                  in1=ebuf[:, 0:128], op=ALU.mult),
                 reads=[psA, ebuf], writes=[vt])
            s.op("dve", lambda e: e.tensor_tensor(out=vt[:, 512:768], in0=psB[:, 0:256],
                                                  in1=ebuf[:, 128:384], op=ALU.mult),
                 reads=[psB, ebuf], writes=[vt])
            out_toks.append(s.dma("pool", vg[ti * 128:(ti + 1) * 128, :], vt[:], vds[ti % 2], reads=[vt]))
        if STOP <= 3:
            return out_toks
        qt = qTt[g % 2]
        for cc in range(6):
            pf = psF[cc % 2]
            s.group("pe", [lambda e, c=c: e.matmul(pf[:], W[:, c, cc * 128:(cc + 1) * 128], hT[:, c, :],
                                                   start=(c == 0), stop=(c == NCH - 1))
                           for c in range(NCH)], reads=[W, hT], writes=[pf])
            if cc < 4:
                s.op("act", lambda e: e.activation(out=sq[:], in_=pf[:], func=AF.Square),
                     reads=[pf], writes=[sq])
                s.op("pe", lambda e: e.matmul(psM[:], ones_inv[:], sq[:], start=True, stop=True),
                     reads=[ones_inv, sq], writes=[psM])
                s.op("act", lambda e: e.activation(out=lnv[:], in_=psM[:], func=AF.Ln, bias=EPS),
                     reads=[psM], writes=[lnv])
                s.op("act", lambda e: e.activation(out=rstd[:], in_=lnv[:], func=AF.Exp, scale=-0.5),
                     reads=[lnv], writes=[rstd])
                s.op("dve", lambda e: e.scalar_tensor_tensor(out=qt[:, cc, :], in0=pf[:],
                                                             scalar=gcol[:, cc:cc + 1], in1=rstd[:],
                                                             op0=ALU.mult, op1=ALU.mult),
                     reads=[pf, gcol, rstd], writes=[qt])
            else:
                s.op("dve", lambda e: e.tensor_scalar(out=qt[:, cc, :], in0=pf[:],
                                                      scalar1=gcol[:, cc:cc + 1], scalar2=None,
                                                      op0=ALU.mult), reads=[pf, gcol], writes=[qt])
        out_toks.append(s.dma("pool", qT[:, :, g * 512:(g + 1) * 512].rearrange("c p t -> p c t"),
                              qt[:], qds[g % 2], reads=[qt]))

    if STOP <= 4:
        return out_toks
    tri = s.sb("tri", [128, 128], BF16)
    su = s.sb("su", [128, 128], BF16)
    onesb = s.sb("onesb", [128, 128], BF16)
    sel = s.sb("sel", [128, 128], BF16)
    totrep = s.sb("totrep", [128, 128], F32)
    cpos = s.sb("cpos_sb", [128, 128], F32)
    crefs = s.sb("cref_sb", [128, 128], F32)
    s.op("pool", lambda e: e.memset(onesb[:], 1.0), writes=[onesb])
    s.op("pool", lambda e: e.memset(tri[:], 1.0), writes=[tri])
    s.op("pool", lambda e: e.memset(su[:], 1.0), writes=[su])
    s.op("pool", lambda e: e.memset(sel[:], 1.0), writes=[sel])
    s.op("pool", lambda e: e.affine_select(out=tri[:], in_=tri[:], pattern=[[1, 128]],
                                           compare_op=ALU.is_ge, fill=0.0, base=0, channel_multiplier=-1),
         reads=[tri], writes=[tri])
    s.op("pool", lambda e: e.affine_select(out=su[:], in_=su[:], pattern=[[1, 128]],
                                           compare_op=ALU.is_gt, fill=0.0, base=0, channel_multiplier=-1),
         reads=[su], writes=[su])
    s.op("pool", lambda e: e.affine_select(out=sel[:], in_=sel[:], pattern=[[0, 128]],
                                           compare_op=ALU.is_ge, fill=0.0, base=-127, channel_multiplier=1),
         reads=[sel], writes=[sel])
    sp3 = [s.sb(f"sp3_{i}", [128, 128], BF16) for i in range(3)]
    spr = s.sb("spr", [128, 128], F32)

    def split3(src):
        s.op("dve", lambda e: e.tensor_copy(sp3[0][:], src[:]), reads=[src], writes=[sp3[0]])
        s.op("dve", lambda e: e.tensor_tensor(out=spr[:], in0=src[:], in1=sp3[0][:], op=ALU.subtract),
             reads=[src, sp3[0]], writes=[spr])
        s.op("dve", lambda e: e.tensor_copy(sp3[1][:], spr[:]), reads=[spr], writes=[sp3[1]])
        s.op("dve", lambda e: e.tensor_tensor(out=spr[:], in0=spr[:], in1=sp3[1][:], op=ALU.subtract),
             reads=[spr, sp3[1]], writes=[spr])
        s.op("dve", lambda e: e.tensor_copy(sp3[2][:], spr[:]), reads=[spr], writes=[sp3[2]])

    split3(lfcol)
    s.group("pe", [lambda e, i=i: e.matmul(psM[:, 0:128], sp3[i][:], onesb[:], start=(i == 0), stop=(i == 2))
                   for i in range(3)], reads=sp3 + [onesb], writes=[psM])
    s.op("dve", lambda e: e.tensor_copy(totrep[:], psM[:, 0:128]), reads=[psM], writes=[totrep])
    s.group("pe", [lambda e, i=i: e.matmul(psA[:, 0:128], tri[:], sp3[i][:], start=(i == 0), stop=False)
                   for i in range(3)], reads=sp3 + [tri], writes=[psA])
    split3(totrep)
    s.group("pe", [lambda e, i=i: e.matmul(psA[:, 0:128], sp3[i][:], su[:], start=False, stop=(i == 2))
                   for i in range(3)], reads=sp3 + [su], writes=[psA])
    s.op("dve", lambda e: e.tensor_copy(cpos[:], psA[:, 0:128]), reads=[psA], writes=[cpos])
    split3(cpos)
    s.group("pe", [lambda e, i=i: e.matmul(psB[:, 0:128], sel[:], sp3[i][:], start=(i == 0), stop=(i == 2))
                   for i in range(3)], reads=sp3 + [sel], writes=[psB])
    s.op("dve", lambda e: e.tensor_copy(crefs[:], psB[:, 0:128]), reads=[psB], writes=[crefs])
    out_toks.append(s.dma("sp", cpos_out, cpos[:, 0:NT], s.dsem("o_cpos"), reads=[cpos]))
    out_toks.append(s.dma("sp", cref_out, crefs[:, 0:NT], s.dsem("o_cref"), reads=[crefs]))
    return out_toks


def build_proj(S):
    nc = bass.Bass("TRN2", target_bir_lowering=False)
    NT = S // 128
    x = nc.dram_tensor("x", [S, D], F32, kind="ExternalInput").ap()
    ng = nc.dram_tensor("ng", [1, D], F32, kind="ExternalInput").ap()
    wc = nc.dram_tensor("wc", [D, WC_COLS], F32, kind="ExternalInput").ap()
    qkg = nc.dram_tensor("qkg", [128, 4], F32, kind="ExternalInput").ap()
    bfv = nc.dram_tensor("bfv", [128, 1], F32, kind="ExternalInput").ap()
    qT = nc.dram_tensor("qT", [6, 128, S], BF16, kind="ExternalOutput").ap()
    vg = nc.dram_tensor("vg", [S, 768], BF16, kind="ExternalOutput").ap()
    cpos = nc.dram_tensor("cpos", [128, NT], F32, kind="ExternalOutput").ap()
    cref = nc.dram_tensor("cref", [128, NT], F32, kind="ExternalOutput").ap()
    with ExitStack() as es:
        s = Sch(nc, es)
        C = _consts(s)
        toks = emit_proj(s, S, x, ng, wc, qkg, bfv, qT, vg, cpos, cref, C)
        s.finish(toks)
        print("proj instrs", s.ninstr)
    return nc


def proj_inputs(x_b, norm_g_l, w_in_l, b_f_l, qk_norm_g_l, h):
    cols = []
    for n, part in ((0, 0), (0, 1), (1, 0), (1, 1), (2, 0), (2, 1)):
        base = n * 4 * 512 + part * 512 + h * 128
        cols.append(np.arange(base, base + 128))
    for n in range(3):
        base = n * 4 * 512 + 2 * 512 + h * 128
        cols.append(np.arange(base, base + 128))
    for n in range(3):
        base = n * 4 * 512 + 3 * 512 + h * 128
        cols.append(np.arange(base, base + 128))
    cols.append(np.array([QKVG + h]))
    cols = np.concatenate(cols)
    wc = np.ascontiguousarray(w_in_l[:, cols])
    qkg = np.ascontiguousarray(qk_norm_g_l.T)
    bfv = np.full((128, 1), b_f_l[h], np.float32)
    return {"x": np.ascontiguousarray(x_b), "ng": np.ascontiguousarray(norm_g_l[None, :]), "wc": wc,
            "qkg": qkg, "bfv": bfv}


def attn_consts_host():
    k = np.arange(128)[:, None]
    q = np.arange(512)[None, :]
    mA = np.stack([(q >= k + 128 * j) for j in range(4)]).astype(np.float32)
    mC = np.stack([(q > k + 128 * j) for j in range(4)]).astype(np.float32)
    mB = []
    for j in range(-4, 4):
        kk = 128 * j + k
        cd = (q // 64) - np.floor_divide(kk, 64)
        mB.append(((cd >= 0) & (cd <= 8)).astype(np.float32))
    mB = np.stack(mB)
    bf = ml_dtypes.bfloat16
    return mA.astype(bf), mB.astype(bf), mC.astype(bf)


def relT_host(rel_bias_h):
    k = np.arange(128)[:, None]
    q = np.arange(512)[None, :]
    out = []
    for j in range(-4, 4):
        diff = q - (128 * j + k)
        idx = np.clip(diff, -128, 128) + 128
        out.append(rel_bias_h[idx])
    return np.ascontiguousarray(np.stack(out).astype(np.float32))


def emit_attn(s, S, qT, vg, cpos_d, cref_d, relT_d, mA_d, mB_d, mC_d, ysT, C, mixers=(0, 1, 2)):
    NT = S // 128
    NB = S // 512
    ident = C["ident"]
    kT = s.sb("kT", [128, S], BF16)
    vS = s.sb("vS", [128, NT, 129], BF16)
    cpos = s.sb("a_cpos", [128, NT], F32)
    cref = s.sb("a_cref", [128, NT], F32)
    relT = s.sb("a_relT", [128, 8, 512], F32)
    mA = s.sb("a_mA", [128, 4, 512], BF16)
    mB = s.sb("a_mB", [128, 8, 512], BF16)
    mC = s.sb("a_mC", [128, 4, 512], BF16)
    onesb = s.sb("a_ones", [128, 128], BF16)
    nu = s.sb("a_nu", [128, 128], BF16)
    negones = s.sb("a_negones", [128, 128], BF16)
    qt = [s.sb(f"a_qt{i}", [128, 512], BF16) for i in range(2)]
    qtd = [s.dsem(f"a_qtd{i}") for i in range(2)]
    sgt = [s.sb(f"a_sg{i}", [128, 4, 128], BF16) for i in range(2)]
    sgd = [s.dsem(f"a_sgd{i}") for i in range(2)]
    bias = [s.sb(f"a_bias{i}", [128, NT], F32) for i in range(2)]
    P = [s.sb(f"a_P{i}", [128, 512], BF16) for i in range(3)]
    L = [s.sb(f"a_L{i}", [128, 512], BF16) for i in range(3)]
    Eb = [s.sb(f"a_E{i}", [128, 512], F32) for i in range(2)]
    tmp = [s.sb(f"a_tmp{i}", [128, 512], F32) for i in range(3)]
    carry = s.sb("a_carry", [128, 512], F32)
    rden = s.sb("a_rden", [128, 4], F32)
    ybf = s.sb("a_ybf", [128, 4, 128], BF16)
    yT = [s.sb(f"a_yT{i}", [128, 512], BF16) for i in range(2)]
    yTd = [s.dsem(f"a_yTd{i}") for i in range(2)]
    psS = [s.ps(f"a_psS{i}", [128, 512], F32) for i in range(2)]
    psR = [s.ps(f"a_psR{i}", [128, 512], F32) for i in range(2)]
    psC = s.ps("a_psC", [128, 512], F32)
    psO = s.ps("a_psO", [128, 2, 512], F32)
    psT = s.ps("a_psT", [128, 512], BF16)
    kd = s.dsem("a_kd")
    vd = s.dsem("a_vd")

    s.dma("sp", cpos[:], cpos_d, s.dsem("a_i0"), writes=[cpos])
    s.dma("sp", cref[:], cref_d, s.dsem("a_i1"), writes=[cref])
    s.dma("sp", relT[:], relT_d.rearrange("j k q -> k j q"), s.dsem("a_i2"), writes=[relT])
    s.dma("sp", mA[:], mA_d.rearrange("j k q -> k j q"), s.dsem("a_i3"), writes=[mA])
    s.dma("sp", mB[:], mB_d.rearrange("j k q -> k j q"), s.dsem("a_i4"), writes=[mB])
    s.dma("sp", mC[:], mC_d.rearrange("j k q -> k j q"), s.dsem("a_i5"), writes=[mC])
    s.op("pool", lambda e: e.memset(onesb[:], 1.0), writes=[onesb])
    s.op("pool", lambda e: e.memset(negones[:], -1.0), writes=[negones])
    s.op("pool", lambda e: e.memset(nu[:], -1.0), writes=[nu])
    s.op("pool", lambda e: e.affine_select(out=nu[:], in_=nu[:], pattern=[[-1, 128]],
                                           compare_op=ALU.is_ge, fill=0.0, base=0, channel_multiplier=1),
         reads=[nu], writes=[nu])
    s.op("pool", lambda e: e.memset(vS[:, :, 128:129], 1.0), writes=[vS])

    out_toks = []
    nblk = [0]
    nload = [0]

    def load_kv(m):
        kidx = 2 * m + 1
        s.dma("sp", kT[:], qT[kidx, :, :], kd, writes=[kT])
        vsrc = vg[:, m * 128:(m + 1) * 128].rearrange("(t p) d -> p t d", p=128)
        for t0 in range(0, NT, 8):
            t1 = min(NT, t0 + 8)
            s.dma("sp", vS[:, t0:t1, 0:128], vsrc[:, t0:t1, :], vd, writes=[vS])

    def load_q(m, qb):
        i = nload[0] % 2
        nload[0] += 1
        s.dma("sp", qt[i][:], qT[2 * m, :, qb * 512:(qb + 1) * 512], qtd[i], writes=[qt[i]])
        s.dma("sp", sgt[i][:], vg[qb * 512:(qb + 1) * 512, 384 + m * 128:384 + (m + 1) * 128]
              .rearrange("(q p) d -> p q d", p=128), sgd[i], writes=[sgt[i]])
        return qt[i], sgt[i]

    def po(qs, w):
        return psO[:, qs // 2, (qs % 2) * 256:(qs % 2) * 256 + w]

    def epilogue(m, qb, sg, softmax):
        i = nblk[0] % 2
        if softmax:
            for qs in range(4):
                s.op("dve", lambda e, qs=qs: e.reciprocal(out=rden[:, qs:qs + 1], in_=po(qs, 129)[:, 128:129]),
                     reads=[psO], writes=[rden])
            for qs in range(4):
                s.op("dve", lambda e, qs=qs: e.scalar_tensor_tensor(
                    out=ybf[:, qs, :], in0=po(qs, 128), scalar=rden[:, qs:qs + 1], in1=sg[:, qs, :],
                    op0=ALU.mult, op1=ALU.mult), reads=[psO, rden, sg], writes=[ybf])
        else:
            for qs in range(4):
                s.op("dve", lambda e, qs=qs: e.tensor_tensor(out=ybf[:, qs, :], in0=po(qs, 128),
                                                            in1=sg[:, qs, :], op=ALU.mult),
                     reads=[psO, sg], writes=[ybf])
        s.group("pe", [lambda e, qs=qs: e.transpose(psT[:, qs * 128:(qs + 1) * 128], ybf[:, qs, :], ident[:])
                       for qs in range(4)], reads=[ybf, ident], writes=[psT])
        s.op("dve", lambda e: e.tensor_copy(yT[i][:], psT[:]), reads=[psT], writes=[yT[i]])
        out_toks.append(s.dma("pool", ysT[m, :, qb * 512:(qb + 1) * 512], yT[i][:], yTd[i], reads=[yT[i]]))
        nblk[0] += 1

    def softmax_mixer(m):
        load_kv(m)
        tiles = []
        for qb in range(NB):
            kts = list(range(0, 4 * qb + 4)) if m == 0 else list(range(max(0, 4 * qb - 4), 4 * qb + 4))
            for n, kt in enumerate(kts):
                tiles.append((qb, kt, kt - 4 * qb, n == 0, n == len(kts) - 1))
        st = {}

        def stage1(i):
            qb, kt, j, first, last = tiles[i]
            if first:
                st["q"], st["sg"] = load_q(m, qb)
                if m == 0:
                    b = bias[qb % 2]
                    nk = 4 * qb + 4
                    s.op("dve", lambda e: e.tensor_scalar(out=b[:, 0:nk], in0=cpos[:, 0:nk],
                                                          scalar1=cref[:, 4 * qb + 1:4 * qb + 2], scalar2=None,
                                                          op0=ALU.subtract), reads=[cpos, cref], writes=[b])
            q = st["q"]
            tiles[i] = tiles[i] + (q, st["sg"])
            s.op("pe", lambda e: e.matmul(psS[i % 2][:], kT[:, kt * 128:(kt + 1) * 128], q[:],
                                          start=True, stop=True), reads=[kT, q], writes=[psS[i % 2]])

        def stage2(i):
            qb, kt, j, first, last, q, sg = tiles[i]
            p = P[i % 3]
            if m == 0:
                b = bias[qb % 2]
                s.op("act", lambda e: e.activation(out=p[:], in_=psS[i % 2][:], func=AF.Exp,
                                                   bias=b[:, kt:kt + 1]), reads=[psS[i % 2], b], writes=[p])
                if j >= 0:
                    s.op("pool", lambda e: e.tensor_tensor(out=p[:], in0=p[:], in1=mA[:, j, :], op=ALU.mult),
                         reads=[p, mA], writes=[p])
            else:
                t = tmp[i % 3]
                s.op("dve", lambda e: e.tensor_tensor(out=t[:], in0=psS[i % 2][:], in1=relT[:, j + 4, :],
                                                      op=ALU.add), reads=[psS[i % 2], relT], writes=[t])
                s.op("act", lambda e: e.activation(out=p[:], in_=t[:], func=AF.Exp), reads=[t], writes=[p])
                s.op("pool", lambda e: e.tensor_tensor(out=p[:], in0=p[:], in1=mB[:, j + 4, :], op=ALU.mult),
                     reads=[p, mB], writes=[p])
            s.group("pe", [lambda e, qs=qs: e.matmul(po(qs, 129), p[:, qs * 128:(qs + 1) * 128], vS[:, kt, :],
                                                     start=(first and qs % 2 == 0), stop=last,
                                                     skip_group_check=True) for qs in range(4)],
                    reads=[p, vS], writes=[psO])
            if last:
                epilogue(m, qb, sg, True)

        n = len(tiles)
        for i in range(n + 1):
            if i < n:
                stage1(i)
            if i >= 1:
                stage2(i - 1)

    def sb_mixer():
        m = 2
        load_kv(m)
        tiles = []
        for qb in range(NB):
            kts = list(range(4 * qb + 3, -1, -1))
            for n, kt in enumerate(kts):
                tiles.append((qb, kt, kt - 4 * qb, n == 0, n == len(kts) - 1))
        st = {}

        def stZ(i):
            qb, kt, j, first, last = tiles[i]
            if first:
                st["q"], st["sg"] = load_q(m, qb)
            q = st["q"]
            tiles[i] = tiles[i] + (q, st["sg"])
            s.op("pe", lambda e: e.matmul(psS[i % 2][:], kT[:, kt * 128:(kt + 1) * 128], q[:],
                                          start=True, stop=True), reads=[kT, q], writes=[psS[i % 2]])

        def stEL(i):
            qb, kt, j, first, last, q, sg = tiles[i]
            E = Eb[i % 2]
            l = L[i % 3]
            s.op("act", lambda e: e.activation(out=E[:], in_=psS[i % 2][:], func=AF.Exp),
                 reads=[psS[i % 2]], writes=[E])
            s.op("act", lambda e: e.activation(out=l[:], in_=E[:], func=AF.Ln, bias=1.0), reads=[E], writes=[l])
            if j >= 0:
                s.op("pool", lambda e: e.tensor_tensor(out=l[:], in0=l[:], in1=mC[:, j, :], op=ALU.mult),
                     reads=[l, mC], writes=[l])

        def stR(i):
            qb, kt, j, first, last, q, sg = tiles[i]
            l = L[i % 3]
            r = psR[i % 2]
            s.group("pe", [lambda e: e.matmul(r[:], nu[:], l[:], start=True, stop=False),
                           lambda e: e.matmul(r[:], kT[:, kt * 128:(kt + 1) * 128], q[:], start=False, stop=True)],
                    reads=[nu, l, kT, q], writes=[r])
            t = tmp[i % 3]
            if first:
                s.op("dve", lambda e: e.tensor_copy(t[:], r[:]), reads=[r], writes=[t])
            else:
                s.op("dve", lambda e: e.tensor_tensor(out=t[:], in0=r[:], in1=carry[:], op=ALU.add),
                     reads=[r, carry], writes=[t])
            if not last:
                s.op("pe", lambda e: e.matmul(psC[:], negones[:], l[:], start=True, stop=True),
                     reads=[negones, l], writes=[psC])
                if first:
                    s.op("dve", lambda e: e.tensor_copy(carry[:], psC[:]), reads=[psC], writes=[carry])
                else:
                    s.op("dve", lambda e: e.tensor_tensor(out=carry[:], in0=psC[:], in1=carry[:], op=ALU.add),
                         reads=[psC, carry], writes=[carry])

        def stA(i):
            qb, kt, j, first, last, q, sg = tiles[i]
            p = P[i % 3]
            t = tmp[i % 3]
            s.op("act", lambda e: e.activation(out=p[:], in_=t[:], func=AF.Exp), reads=[t], writes=[p])
            if j >= 0:
                s.op("pool", lambda e: e.tensor_tensor(out=p[:], in0=p[:], in1=mC[:, j, :], op=ALU.mult),
                     reads=[p, mC], writes=[p])
            s.group("pe", [lambda e, qs=qs: e.matmul(po(qs, 128), p[:, qs * 128:(qs + 1) * 128], vS[:, kt, 0:128],
                                                     start=(first and qs % 2 == 0), stop=last,
                                                     skip_group_check=True) for qs in range(4)],
                    reads=[p, vS], writes=[psO])
            if last:
                epilogue(m, qb, sg, False)

        n = len(tiles)
        stZ(0)
        for i in range(n + 2):
            if i + 1 < n:
                stZ(i + 1)
            if i < n:
                stEL(i)
            if 1 <= i <= n:
                stR(i - 1)
            if i >= 2:
                stA(i - 2)

    for m in mixers:
        if m in (0, 1):
            softmax_mixer(m)
        else:
            sb_mixer()
    return out_toks


def build_attn(S, mixers=(0, 1, 2)):
    nc = bass.Bass("TRN2", target_bir_lowering=False)
    NT = S // 128
    qT = nc.dram_tensor("qT", [6, 128, S], BF16, kind="ExternalInput").ap()
    vg = nc.dram_tensor("vg", [S, 768], BF16, kind="ExternalInput").ap()
    cpos = nc.dram_tensor("cpos", [128, NT], F32, kind="ExternalInput").ap()
    cref = nc.dram_tensor("cref", [128, NT], F32, kind="ExternalInput").ap()
    relT = nc.dram_tensor("relT", [8, 128, 512], F32, kind="ExternalInput").ap()
    mA = nc.dram_tensor("mA", [4, 128, 512], BF16, kind="ExternalInput").ap()
    mB = nc.dram_tensor("mB", [8, 128, 512], BF16, kind="ExternalInput").ap()
    mC = nc.dram_tensor("mC", [4, 128, 512], BF16, kind="ExternalInput").ap()
    ysT = nc.dram_tensor("ysT", [3, 128, S], BF16, kind="ExternalOutput").ap()
    with ExitStack() as es:
        s = Sch(nc, es)
        C = _consts(s)
        toks = emit_attn(s, S, qT, vg, cpos, cref, relT, mA, mB, mC, ysT, C, mixers)
        s.finish(toks)
        print("attn instrs", s.ninstr)
    return nc


def emit_merge(s, T, x, ng, wg, bg, wup, wout, ysT, y, C, TB=1024):
    ident = C["ident"]
    NBLK = T // TB
    NTT = TB // 128
    NTG = TB // 512
    gB = s.sb("m_gB", [128, D], F32)
    bgs = s.sb("m_bg", [128, 48], F32)
    xt = [s.sb(f"m_xt{i}", [128, D], F32) for i in range(2)]
    xds = [s.dsem(f"m_xds{i}") for i in range(2)]
    junk = s.sb("m_junk", [128, D], BF16)
    ss = s.sb("m_ss", [128, 4], F32)
    hn = [s.sb(f"m_hn{i}", [128, D], BF16) for i in range(2)]
    hnT = s.sb("m_hnT", [128, NCH, TB], BF16)
    ys = s.sb("m_ys", [128, 12, TB], BF16)
    ysd = s.dsem("m_ysd")
    mT = s.sb("m_mT", [128, NCH, TB], BF16)
    wgc = [s.sb(f"m_wgc{i}", [128, NCH, 128], BF16) for i in range(2)]
    wgd = [s.dsem(f"m_wgd{i}") for i in range(2)]
    wuc = [s.sb(f"m_wuc{i}", [128, 4, 128], BF16) for i in range(2)]
    wud = [s.dsem(f"m_wud{i}") for i in range(2)]
    woc = [s.sb(f"m_woc{i}", [128, NCH, 512], BF16) for i in range(2)]
    wod = [s.dsem(f"m_wod{i}") for i in range(2)]
    sig = [s.sb(f"m_sig{i}", [128, 512], F32) for i in range(2)]
    prod = [s.sb(f"m_prod{i}", [128, 512], F32) for i in range(2)]
    acc = [s.sb(f"m_acc{i}", [128, 512], F32) for i in range(NTG)]
    xo = [s.sb(f"m_xo{i}", [128, 512], F32) for i in range(3)]
    xod = [s.dsem(f"m_xod{i}") for i in range(3)]
    xsd = [s.dsem(f"m_xsd{i}") for i in range(3)]
    psT = s.ps("m_psT", [128, D], BF16)
    psG = [s.ps(f"m_psG{i}", [128, 512], F32) for i in range(2)]
    psU = [s.ps(f"m_psU{i}", [128, 512], F32) for i in range(2)]
    psY = [s.ps(f"m_psY{i}", [128, 512], F32) for i in range(2)]

    s.dma("sp", gB[:], ng.partition_broadcast(128), s.dsem("m_i0"), writes=[gB])
    s.dma("sp", bgs[:], bg, s.dsem("m_i1"), writes=[bgs])
    out_toks = []
    nw = 0
    nwo = 0
    nx = 0
    nxo = 0
    for blk in range(NBLK):
        t0 = blk * TB
        for n in range(3):
            s.dma("sp", ys[:, n * 4:(n + 1) * 4, :], ysT[n, :, :, t0:t0 + TB].rearrange("h p t -> p h t"),
                  ysd, writes=[ys])
        for tt in range(NTT):
            xs = xt[nx % 2]
            s.dma("sp", xs[:], x[t0 + tt * 128:t0 + (tt + 1) * 128, :], xds[nx % 2], writes=[xs])
            s.op("pool", lambda e: e.memset(ss[:, 0:1], 0.0), writes=[ss])
            s.op("act", lambda e: e.activation(out=junk[:], in_=xs[:], func=AF.Square, accum_out=ss[:, 0:1]),
                 reads=[xs], writes=[junk, ss])
            s.op("act", lambda e: e.activation(out=ss[:, 1:2], in_=ss[:, 0:1], func=AF.Ln,
                                               scale=1.0 / D, bias=EPS), reads=[ss], writes=[ss])
            s.op("act", lambda e: e.activation(out=ss[:, 2:3], in_=ss[:, 1:2], func=AF.Exp, scale=-0.5),
                 reads=[ss], writes=[ss])
            h = hn[nx % 2]
            s.op("dve", lambda e: e.scalar_tensor_tensor(out=h[:], in0=xs[:], scalar=ss[:, 2:3], in1=gB[:],
                                                         op0=ALU.mult, op1=ALU.mult),
                 reads=[xs, ss, gB], writes=[h])
            s.group("pe", [lambda e, c=c: e.transpose(psT[:, c * 128:(c + 1) * 128],
                                                      h[:, c * 128:(c + 1) * 128], ident[:])
                           for c in range(NCH)], reads=[h, ident], writes=[psT])
            pv = psT.ap[:].rearrange("p (c t) -> p c t", c=NCH)
            s.op("act", lambda e: e.copy(out=hnT[:, 0:8, tt * 128:(tt + 1) * 128], in_=pv[:, 0:8, :]),
                 reads=[psT], writes=[hnT])
            s.op("dve", lambda e: e.tensor_copy(hnT[:, 8:16, tt * 128:(tt + 1) * 128], pv[:, 8:16, :]),
                 reads=[psT], writes=[hnT])
            nx += 1
        for dc in range(NCH):
            for n in range(3):
                wgb = wgc[nw % 2]
                wub = wuc[nw % 2]
                col = n * D + dc * 128
                s.dma("pool", wgb[:], wg[:, col:col + 128].rearrange("(c p) n -> p c n", p=128),
                      wgd[nw % 2], writes=[wgb])
                s.dma("pool", wub[:], wup[n, :, dc * 128:(dc + 1) * 128].rearrange("(k p) n -> p k n", p=128),
                      wud[nw % 2], writes=[wub])
                nw += 1
                for tg in range(NTG):
                    pg = psG[tg % 2]
                    pu = psU[tg % 2]
                    s.group("pe", [lambda e, c=c: e.matmul(pg[:], wgb[:, c, :], hnT[:, c, tg * 512:(tg + 1) * 512],
                                                           start=(c == 0), stop=(c == NCH - 1))
                                   for c in range(NCH)], reads=[wgb, hnT], writes=[pg])
                    s.group("pe", [lambda e, k=k: e.matmul(pu[:], wub[:, k, :],
                                                           ys[:, n * 4 + k, tg * 512:(tg + 1) * 512],
                                                           start=(k == 0), stop=(k == 3))
                                   for k in range(4)], reads=[wub, ys], writes=[pu])
                    sg = sig[tg % 2]
                    s.op("act", lambda e: e.activation(out=sg[:], in_=pg[:], func=AF.Sigmoid,
                                                       bias=bgs[:, n * 16 + dc:n * 16 + dc + 1]),
                         reads=[pg, bgs], writes=[sg])
                    a = acc[tg]
                    if n == 0:
                        s.op("dve", lambda e: e.tensor_tensor(out=a[:], in0=pu[:], in1=sg[:], op=ALU.mult),
                             reads=[pu, sg], writes=[a])
                    else:
                        pr = prod[tg % 2]
                        s.op("dve", lambda e: e.tensor_tensor(out=pr[:], in0=pu[:], in1=sg[:], op=ALU.mult),
                             reads=[pu, sg], writes=[pr])
                        if n == 1:
                            s.op("pool", lambda e: e.tensor_tensor(out=a[:], in0=a[:], in1=pr[:], op=ALU.add),
                                 reads=[a, pr], writes=[a])
                        else:
                            s.op("pool", lambda e: e.tensor_tensor(out=mT[:, dc, tg * 512:(tg + 1) * 512],
                                                                   in0=a[:], in1=pr[:], op=ALU.add),
                                 reads=[a, pr], writes=[mT])
        for cg in range(4):
            wo = woc[nwo % 2]
            s.dma("pool", wo[:], wout[:, cg * 512:(cg + 1) * 512].rearrange("(c p) n -> p c n", p=128),
                  wod[nwo % 2], writes=[wo])
            nwo += 1
            for tt in range(NTT):
                xb = xo[nxo % 3]
                rows = slice(t0 + tt * 128, t0 + (tt + 1) * 128)
                s.dma("sp", xb[:], x[rows, cg * 512:(cg + 1) * 512], xod[nxo % 3], writes=[xb])
                py = psY[nxo % 2]
                s.group("pe", [lambda e, c=c: e.matmul(py[:], mT[:, c, tt * 128:(tt + 1) * 128], wo[:, c, :],
                                                       start=(c == 0), stop=(c == NCH - 1))
                               for c in range(NCH)], reads=[mT, wo], writes=[py])
                s.op("dve", lambda e: e.tensor_tensor(out=xb[:], in0=py[:], in1=xb[:], op=ALU.add),
                     reads=[py, xb], writes=[xb])
                out_toks.append(s.dma("sp", y[rows, cg * 512:(cg + 1) * 512], xb[:], xsd[nxo % 3], reads=[xb]))
                nxo += 1
    return out_toks


def build_merge(T, TB=1024):
    nc = bass.Bass("TRN2", target_bir_lowering=False)
    x = nc.dram_tensor("x", [T, D], F32, kind="ExternalInput").ap()
    ng = nc.dram_tensor("ng", [1, D], F32, kind="ExternalInput").ap()
    wg = nc.dram_tensor("wg", [D, 3 * D], F32, kind="ExternalInput").ap()
    bg = nc.dram_tensor("bg", [128, 48], F32, kind="ExternalInput").ap()
    wup = nc.dram_tensor("wup", [3, 512, D], F32, kind="ExternalInput").ap()
    wout = nc.dram_tensor("wout", [D, D], F32, kind="ExternalInput").ap()
    ysT = nc.dram_tensor("ysT", [3, 4, 128, T], BF16, kind="ExternalInput").ap()
    y = nc.dram_tensor("y", [T, D], F32, kind="ExternalOutput").ap()
    with ExitStack() as es:
        s = Sch(nc, es)
        C = _consts(s)
        toks = emit_merge(s, T, x, ng, wg, bg, wup, wout, ysT, y, C, TB)
        s.finish(toks)
        print("merge instrs", s.ninstr)
    return nc


BATCH = 2
SEQ = 16384
DEPTH = 2
_PROGS = {}


def _prog(name, fn, *a):
    key = (name,) + a
    if key not in _PROGS:
        _PROGS[key] = fn(*a)
    return _PROGS[key]


def kernel(x, norm_g, w_in, b_f, b_gate, qk_norm_g, rel_bias, w_up, w_out):
    x = np.asarray(x, np.float32)
    norm_g = np.asarray(norm_g, np.float32)
    w_in = np.asarray(w_in, np.float32)
    b_f = np.asarray(b_f, np.float32)
    b_gate = np.asarray(b_gate, np.float32)
    qk_norm_g = np.asarray(qk_norm_g, np.float32)
    rel_bias = np.asarray(rel_bias, np.float32)
    w_up = np.asarray(w_up, np.float32)
    w_out = np.asarray(w_out, np.float32)
    cores = list(range(8))
    mA, mB, mC = attn_consts_host()
    TQ = SEQ // 4
    cur = x
    for l in range(DEPTH):
        ncp = _prog("proj", build_proj, SEQ)
        ims = [proj_inputs(cur[c // 4], norm_g[l], w_in[l], b_f[l], qk_norm_g[l], c % 4) for c in cores]
        rp = run_bass_kernel_spmd(ncp, ims, core_ids=cores).results
        del ims
        nca = _prog("attn", build_attn, SEQ)
        ims = [{"qT": rp[c]["qT"], "vg": rp[c]["vg"], "cpos": rp[c]["cpos"], "cref": rp[c]["cref"],
                "relT": relT_host(rel_bias[l, c % 4]), "mA": mA, "mB": mB, "mC": mC} for c in cores]
        ra = run_bass_kernel_spmd(nca, ims, core_ids=cores).results
        del ims, rp
        ncm = _prog("merge", build_merge, TQ)
        wg = np.ascontiguousarray(w_in[l][:, QKVG + 4:])
        bg = np.ascontiguousarray(b_gate[l].reshape(48, 128).T)
        ims = []
        for c in cores:
            b, tq = c // 4, c % 4
            ysT = np.ascontiguousarray(np.stack([ra[b * 4 + h]["ysT"][:, :, tq * TQ:(tq + 1) * TQ]
                                                 for h in range(4)], axis=1))
            ims.append({"x": np.ascontiguousarray(cur[b, tq * TQ:(tq + 1) * TQ]), "ng": norm_g[l][None, :],
                        "wg": wg, "bg": bg, "wup": w_up[l], "wout": w_out[l], "ysT": ysT})
        rm = run_bass_kernel_spmd(ncm, ims, core_ids=cores).results
        del ims, ra
        nxt = np.empty_like(cur)
        for c in cores:
            b, tq = c // 4, c % 4
            nxt[b, tq * TQ:(tq + 1) * TQ] = rm[c]["y"]
        cur = nxt
    return cur
```

```python
import numpy as np
from contextlib import ExitStack
import ml_dtypes
import concourse.bass as bass
import concourse.mybir as mybir
from concourse.bass_utils import run_bass_kernel_spmd

F32 = mybir.dt.float32
BF16 = mybir.dt.bfloat16
AF = mybir.ActivationFunctionType
ALU = mybir.AluOpType
AX = mybir.AxisListType

D = 2048
NCH = 16
HD = 128
EPS = 1e-6
SCALE = HD ** -0.5
QKVG = 6144
N_HEADS = 4
FM_COLS = 768
TM_COLS = 769
WC_COLS = FM_COLS + TM_COLS


class Buf:
    __slots__ = ("ap", "w", "r", "name")

    def __init__(self, ap, name=""):
        self.ap = ap
        self.w = None
        self.r = []
        self.name = name

    def __getitem__(self, idx):
        return self.ap[idx]


class DSem:
    def __init__(self, sem, name):
        self.sem = sem
        self.n = 0
        self.name = name


class Sch:
    def __init__(self, nc, es):
        self.nc = nc
        self.es = es
        self.eng = {"pe": nc.tensor, "act": nc.scalar, "dve": nc.vector, "pool": nc.gpsimd,
                    "sp": nc.sync}
        self.sem = {k: es.enter_context(nc.semaphore("prog_" + k)) for k in self.eng}
        self.cnt = {k: 0 for k in self.eng}
        self.seen = {k: {} for k in self.eng}
        self.nd = 0
        self.ninstr = 0

    def dsem(self, name=None):
        self.nd += 1
        name = name or f"dma{self.nd}"
        return DSem(self.es.enter_context(self.nc.semaphore(name)), name)

    def sb(self, name, shape, dt):
        return Buf(self.es.enter_context(self.nc.sbuf_tensor(name, shape, dt)), name)

    def ps(self, name, shape, dt):
        return Buf(self.es.enter_context(self.nc.psum_tensor(name, shape, dt)), name)

    def _wait(self, e, toks):
        seen = self.seen[e]
        for t in toks:
            if t is None:
                continue
            key, sem, val = t
            if key == "pe" and e == "pe":
                continue
            if seen.get(key, 0) >= val:
                continue
            self.eng[e].wait_ge(sem, val)
            seen[key] = val

    def _deps(self, reads, writes):
        deps = []
        for b in reads:
            deps.append(b.w)
        for b in writes:
            deps.append(b.w)
            deps.extend(b.r)
        return deps

    def _commit(self, tok, reads, writes):
        for b in writes:
            b.w = tok
            b.r = []
        for b in reads:
            if b not in writes:
                b.r.append(tok)
                if len(b.r) > 24:
                    b.r = b.r[-24:]

    def op(self, e, fn, reads=(), writes=(), extra=()):
        self._wait(e, self._deps(reads, writes) + list(extra))
        ins = fn(self.eng[e])
        self.cnt[e] += 1
        ins.then_inc(self.sem[e], 1)
        self.ninstr += 1
        tok = (e, self.sem[e], self.cnt[e])
        self._commit(tok, reads, writes)
        return tok

    def group(self, e, fns, reads=(), writes=(), extra=()):
        self._wait(e, self._deps(reads, writes) + list(extra))
        ins = None
        for fn in fns:
            ins = fn(self.eng[e])
            self.ninstr += 1
        self.cnt[e] += 1
        ins.then_inc(self.sem[e], 1)
        tok = (e, self.sem[e], self.cnt[e])
        self._commit(tok, reads, writes)
        return tok

    def dma(self, e, out, in_, ds, reads=(), writes=(), extra=()):
        self._wait(e, self._deps(reads, writes) + list(extra))
        ins = self.eng[e].dma_start(out=out, in_=in_)
        ds.n += 16
        ins.then_inc(ds.sem, 16)
        self.ninstr += 1
        tok = (ds.name, ds.sem, ds.n)
        self._commit(tok, reads, writes)
        return tok

    def finish(self, toks):
        self._wait("sp", toks)

    def barrier(self, toks=()):
        cur = [(k, self.sem[k], self.cnt[k]) for k in self.eng if self.cnt[k] > 0]
        for e in self.eng:
            self._wait(e, [t for t in cur if t[0] != e] + list(toks))

    def scope(self):
        outer = self

        class _Scope:
            def __enter__(self_inner):
                self_inner.saved = outer.es
                self_inner.es = ExitStack()
                self_inner.es.__enter__()
                outer.es = self_inner.es
                return outer

            def __exit__(self_inner, *a):
                outer.es = self_inner.saved
                return self_inner.es.__exit__(*a)
        return _Scope()


def _consts(s):
    c = {}
    c["ident"] = s.sb("ident", [128, 128], BF16)
    s.op("pool", lambda e: e.memset(c["ident"][:], 1.0), writes=[c["ident"]])
    s.op("pool", lambda e: e.affine_select(out=c["ident"][:], in_=c["ident"][:], pattern=[[-1, 128]],
                                           compare_op=ALU.is_equal, fill=0.0, base=0,
                                           channel_multiplier=1),
         reads=[c["ident"]], writes=[c["ident"]])
    return c


def emit_proj(s, S, x, ng, wc, qkg, bfv, qT, vg, cpos_out, cref_out, C):
    nc = s.nc
    NT = S // 128
    NG = S // 512
    ident = C["ident"]
    W = s.sb("W", [128, NCH, WC_COLS], BF16)
    gB = s.sb("gB", [128, D], F32)
    gcol = s.sb("gcol", [128, 6], F32)
    nbf = s.sb("nbf", [128, 1], F32)
    ones_inv = s.sb("ones_inv", [128, 128], BF16)
    xt = [s.sb(f"xt{i}", [128, D], F32) for i in range(3)]
    xds = [s.dsem(f"xds{i}") for i in range(3)]
    junk = s.sb("junk", [128, D], BF16)
    ss = s.sb("ss", [128, 4], F32)
    hn = [s.sb(f"hn{i}", [128, D], BF16) for i in range(2)]
    hnT = [s.sb(f"hnT{i}", [128, NCH, 512], BF16) for i in range(2)]
    vgt = [s.sb(f"vgt{i}", [128, 768], BF16) for i in range(2)]
    vds = [s.dsem(f"vds{i}") for i in range(2)]
    ebuf = s.sb("ebuf", [128, 384], F32)
    ef = s.sb("ef", [128, 2], F32)
    lfcol = s.sb("lfcol", [128, 128], F32)
    sq = s.sb("sq", [128, 512], BF16)
    lnv = s.sb("lnv", [128, 512], F32)
    rstd = s.sb("rstdb", [128, 512], F32)
    qTt = [s.sb(f"qTt{i}", [128, 6, 512], BF16) for i in range(2)]
    qds = [s.dsem(f"qds{i}") for i in range(2)]
    psTa = s.ps("psTa", [128, 1024], BF16)
    psTb = s.ps("psTb", [128, 1024], BF16)
    ss2 = [s.sb(f"ss2_{i}", [128, 4], F32) for i in range(2)]
    ebuf2 = [s.sb(f"ebuf2_{i}", [128, 384], F32) for i in range(2)]
    psA = s.ps("psA", [128, 512], F32)
    psB = s.ps("psB", [128, 512], F32)
    psF = [s.ps(f"psF{i}", [128, 512], F32) for i in range(2)]
    psM = s.ps("psM", [128, 512], F32)
    ids = s.dsem("init")

    for c in range(NCH):
        s.dma("pool", W[:, c, :], wc[c * 128:(c + 1) * 128, :], ids, writes=[W])
    s.dma("sp", gB[:], ng.partition_broadcast(128), s.dsem("i_gB"), writes=[gB])
    s.dma("sp", gcol[:, 0:4], qkg, s.dsem("i_gcol"), writes=[gcol])
    s.dma("sp", nbf[:], bfv, s.dsem("i_nbf"), writes=[nbf])
    s.op("pool", lambda e: e.memset(ones_inv[:], 1.0 / 128.0), writes=[ones_inv])
    s.op("pool", lambda e: e.memset(lfcol[:], 0.0), writes=[lfcol])
    s.op("pool", lambda e: e.memset(gcol[:, 4:6], 1.0), reads=[], writes=[gcol])
    for j in (0, 2, 4):
        s.op("dve", lambda e, j=j: e.tensor_scalar(out=gcol[:, j:j + 1], in0=gcol[:, j:j + 1],
                                                    scalar1=SCALE, scalar2=None, op0=ALU.mult),
             reads=[gcol], writes=[gcol])
    s.op("dve", lambda e: e.tensor_scalar(out=nbf[:], in0=nbf[:], scalar1=-1.0, scalar2=None,
                                          op0=ALU.mult), reads=[nbf], writes=[nbf])

    out_toks = []
    pvA = psTa.ap[:].rearrange("p (c t) -> p c t", c=8)
    pvB = psTb.ap[:].rearrange("p (c t) -> p c t", c=8)

    def stage_load(ti):
        s.dma("sp", xt[ti % 3][:], x[ti * 128:(ti + 1) * 128, :], xds[ti % 3], writes=[xt[ti % 3]])

    def stage_norm(ti):
        g, tt = divmod(ti, 4)
        hT = hnT[g % 2]
        xs = xt[ti % 3]
        sst = ss2[ti % 2]
        s.op("dve", lambda e: e.memset(sst[:, 0:1], 0.0), writes=[sst])
        s.op("act", lambda e: e.activation(out=junk[:], in_=xs[:], func=AF.Square, accum_out=sst[:, 0:1]),
             reads=[xs], writes=[junk, sst])
        s.op("act", lambda e: e.activation(out=sst[:, 1:2], in_=sst[:, 0:1], func=AF.Ln,
                                           scale=1.0 / D, bias=EPS), reads=[sst], writes=[sst])
        s.op("act", lambda e: e.activation(out=sst[:, 2:3], in_=sst[:, 1:2], func=AF.Exp, scale=-0.5),
             reads=[sst], writes=[sst])
        h = hn[ti % 2]
        s.op("dve", lambda e: e.scalar_tensor_tensor(out=h[:], in0=xs[:], scalar=sst[:, 2:3], in1=gB[:],
                                                     op0=ALU.mult, op1=ALU.mult),
             reads=[xs, sst, gB], writes=[h])

    def stage_tr(ti):
        g, tt = divmod(ti, 4)
        hT = hnT[g % 2]
        h = hn[ti % 2]
        s.group("pe", [lambda e, c=c: e.transpose(psTa[:, c * 128:(c + 1) * 128],
                                                  h[:, c * 128:(c + 1) * 128], ident[:])
                       for c in range(8)], reads=[h, ident], writes=[psTa])
        s.group("pe", [lambda e, c=c: e.transpose(psTb[:, c * 128:(c + 1) * 128],
                                                  h[:, (8 + c) * 128:(9 + c) * 128], ident[:])
                       for c in range(8)], reads=[h, ident], writes=[psTb])
        s.op("act", lambda e: e.copy(out=hT[:, 0:8, tt * 128:(tt + 1) * 128], in_=pvA),
             reads=[psTa], writes=[hT])
        s.op("dve", lambda e: e.tensor_copy(hT[:, 8:16, tt * 128:(tt + 1) * 128], pvB),
             reads=[psTb], writes=[hT])

    def stage_tm(ti):
        g, tt = divmod(ti, 4)
        hT = hnT[g % 2]
        fns = []
        for c in range(NCH):
            fns.append(lambda e, c=c: e.matmul(psA[:], hT[:, c, tt * 128:(tt + 1) * 128],
                                               W[:, c, 768:1280], start=(c == 0), stop=(c == NCH - 1)))
        for c in range(NCH):
            fns.append(lambda e, c=c: e.matmul(psB[:, 0:257], hT[:, c, tt * 128:(tt + 1) * 128],
                                               W[:, c, 1280:1537], start=(c == 0), stop=(c == NCH - 1)))
        s.group("pe", fns, reads=[hT, W], writes=[psA, psB])
        vt = vgt[ti % 2]
        eb = ebuf2[ti % 2]
        s.op("act", lambda e: e.copy(out=vt[:, 0:384], in_=psA[:, 0:384]), reads=[psA], writes=[vt])
        s.op("act", lambda e: e.activation(out=eb[:, 0:128], in_=psA[:, 384:512], func=AF.Exp,
                                           scale=-1.0), reads=[psA], writes=[eb])
        s.op("act", lambda e: e.activation(out=eb[:, 128:384], in_=psB[:, 0:256], func=AF.Exp,
                                           scale=-1.0), reads=[psB], writes=[eb])
        s.op("act", lambda e: e.activation(out=ef[:, 0:1], in_=psB[:, 256:257], func=AF.Exp,
                                           scale=-1.0, bias=nbf[:, 0:1]), reads=[psB, nbf], writes=[ef])
        s.op("act", lambda e: e.activation(out=lfcol[:, ti:ti + 1], in_=ef[:, 0:1], func=AF.Ln,
                                           bias=1.0), reads=[ef], writes=[lfcol])
        s.op("act", lambda e: e.activation(out=eb[:], in_=eb[:], func=AF.Ln, bias=1.0), reads=[eb], writes=[eb])
        s.op("act", lambda e: e.activation(out=eb[:], in_=eb[:], func=AF.Exp, scale=-1.0), reads=[eb], writes=[eb])
        s.op("dve", lambda e: e.tensor_tensor(out=vt[:, 384:512], in0=psA[:, 384:512],
                                              in1=eb[:, 0:128], op=ALU.mult),
             reads=[psA, eb], writes=[vt])
        s.op("dve", lambda e: e.tensor_tensor(out=vt[:, 512:768], in0=psB[:, 0:256],
                                              in1=eb[:, 128:384], op=ALU.mult),
             reads=[psB, eb], writes=[vt])
        out_toks.append(s.dma("pool", vg[:, :, ti, :].rearrange("m p d -> p m d"),
                              vt.ap[:].rearrange("p (m d) -> p m d", m=6), vds[ti % 2], reads=[vt]))

    def stage_fm(g, chunks):
        hT = hnT[g % 2]
        qt = qTt[g % 2]
        for cc in chunks:
            pf = psF[cc % 2]
            s.group("pe", [lambda e, c=c: e.matmul(pf[:], W[:, c, cc * 128:(cc + 1) * 128], hT[:, c, :],
                                                   start=(c == 0), stop=(c == NCH - 1))
                           for c in range(NCH)], reads=[W, hT], writes=[pf])
            if cc < 4:
                s.op("act", lambda e: e.activation(out=sq[:], in_=pf[:], func=AF.Square),
                     reads=[pf], writes=[sq])
                s.op("pe", lambda e: e.matmul(psM[:], ones_inv[:], sq[:], start=True, stop=True),
                     reads=[ones_inv, sq], writes=[psM])
                s.op("act", lambda e: e.activation(out=lnv[:], in_=psM[:], func=AF.Ln, bias=EPS),
                     reads=[psM], writes=[lnv])
                s.op("act", lambda e: e.activation(out=rstd[:], in_=lnv[:], func=AF.Exp, scale=-0.5),
                     reads=[lnv], writes=[rstd])
                s.op("dve", lambda e: e.scalar_tensor_tensor(out=qt[:, cc, :], in0=pf[:],
                                                             scalar=gcol[:, cc:cc + 1], in1=rstd[:],
                                                             op0=ALU.mult, op1=ALU.mult),
                     reads=[pf, gcol, rstd], writes=[qt])
            else:
                s.op("dve", lambda e: e.tensor_scalar(out=qt[:, cc, :], in0=pf[:],
                                                      scalar1=gcol[:, cc:cc + 1], scalar2=None,
                                                      op0=ALU.mult), reads=[pf, gcol], writes=[qt])
        if 5 in chunks:
            out_toks.append(s.dma("pool", qT[:, :, g * 512:(g + 1) * 512].rearrange("c p t -> p c t"),
                                  qt[:], qds[g % 2], reads=[qt]))

    for t in range(min(3, NT)):
        stage_load(t)
    stage_norm(0)
    if NT > 1:
        stage_norm(1)
    stage_tr(0)
    NG_ = NT // 4
    for ti in range(NT):
        if ti + 3 < NT:
            stage_load(ti + 3)
        if ti + 2 < NT:
            stage_norm(ti + 2)
        if ti + 1 < NT:
            stage_tr(ti + 1)
        stage_tm(ti)
        g, tt = divmod(ti, 4)
        if g >= 1 and tt < 3:
            stage_fm(g - 1, [2 * tt, 2 * tt + 1])
    stage_fm(NG_ - 1, [0, 1, 2, 3, 4, 5])
    tri = s.sb("tri", [128, 128], BF16)
    su = s.sb("su", [128, 128], BF16)
    onesb = s.sb("onesb", [128, 128], BF16)
    sel = s.sb("sel", [128, 128], BF16)
    totrep = s.sb("totrep", [128, 128], F32)
    cpos = s.sb("cpos_sb", [128, 128], F32)
    crefs = s.sb("cref_sb", [128, 128], F32)
    s.op("pool", lambda e: e.memset(onesb[:], 1.0), writes=[onesb])
    s.op("pool", lambda e: e.memset(tri[:], 1.0), writes=[tri])
    s.op("pool", lambda e: e.memset(su[:], 1.0), writes=[su])
    s.op("pool", lambda e: e.memset(sel[:], 1.0), writes=[sel])
    s.op("pool", lambda e: e.affine_select(out=tri[:], in_=tri[:], pattern=[[1, 128]],
                                           compare_op=ALU.is_ge, fill=0.0, base=0, channel_multiplier=-1),
         reads=[tri], writes=[tri])
    s.op("pool", lambda e: e.affine_select(out=su[:], in_=su[:], pattern=[[1, 128]],
                                           compare_op=ALU.is_gt, fill=0.0, base=0, channel_multiplier=-1),
         reads=[su], writes=[su])
    s.op("pool", lambda e: e.affine_select(out=sel[:], in_=sel[:], pattern=[[0, 128]],
                                           compare_op=ALU.is_ge, fill=0.0, base=-127, channel_multiplier=1),
         reads=[sel], writes=[sel])
    sp3 = [s.sb(f"sp3_{i}", [128, 128], BF16) for i in range(3)]
    spr = s.sb("spr", [128, 128], F32)

    def split3(src):
        s.op("dve", lambda e: e.tensor_copy(sp3[0][:], src[:]), reads=[src], writes=[sp3[0]])
        s.op("dve", lambda e: e.tensor_tensor(out=spr[:], in0=src[:], in1=sp3[0][:], op=ALU.subtract),
             reads=[src, sp3[0]], writes=[spr])
        s.op("dve", lambda e: e.tensor_copy(sp3[1][:], spr[:]), reads=[spr], writes=[sp3[1]])
        s.op("dve", lambda e: e.tensor_tensor(out=spr[:], in0=spr[:], in1=sp3[1][:], op=ALU.subtract),
             reads=[spr, sp3[1]], writes=[spr])
        s.op("dve", lambda e: e.tensor_copy(sp3[2][:], spr[:]), reads=[spr], writes=[sp3[2]])

    split3(lfcol)
    s.group("pe", [lambda e, i=i: e.matmul(psM[:, 0:128], sp3[i][:], onesb[:], start=(i == 0), stop=(i == 2))
                   for i in range(3)], reads=sp3 + [onesb], writes=[psM])
    s.op("dve", lambda e: e.tensor_copy(totrep[:], psM[:, 0:128]), reads=[psM], writes=[totrep])
    s.group("pe", [lambda e, i=i: e.matmul(psA[:, 0:128], tri[:], sp3[i][:], start=(i == 0), stop=False)
                   for i in range(3)], reads=sp3 + [tri], writes=[psA])
    split3(totrep)
    s.group("pe", [lambda e, i=i: e.matmul(psA[:, 0:128], sp3[i][:], su[:], start=False, stop=(i == 2))
                   for i in range(3)], reads=sp3 + [su], writes=[psA])
    s.op("dve", lambda e: e.tensor_copy(cpos[:], psA[:, 0:128]), reads=[psA], writes=[cpos])
    split3(cpos)
    s.group("pe", [lambda e, i=i: e.matmul(psB[:, 0:128], sel[:], sp3[i][:], start=(i == 0), stop=(i == 2))
                   for i in range(3)], reads=sp3 + [sel], writes=[psB])
    s.op("dve", lambda e: e.tensor_copy(crefs[:], psB[:, 0:128]), reads=[psB], writes=[crefs])
    out_toks.append(s.dma("sp", cpos_out, cpos[:, 0:NT], s.dsem("o_cpos"), reads=[cpos]))
    out_toks.append(s.dma("sp", cref_out, crefs[:, 0:NT], s.dsem("o_cref"), reads=[crefs]))
    return out_toks


def build_proj(S):
    nc = bass.Bass("TRN2", target_bir_lowering=False)
    NT = S // 128
    x = nc.dram_tensor("x", [S, D], F32, kind="ExternalInput").ap()
    ng = nc.dram_tensor("ng", [1, D], F32, kind="ExternalInput").ap()
    wc = nc.dram_tensor("wc", [D, WC_COLS], F32, kind="ExternalInput").ap()
    qkg = nc.dram_tensor("qkg", [128, 4], F32, kind="ExternalInput").ap()
    bfv = nc.dram_tensor("bfv", [128, 1], F32, kind="ExternalInput").ap()
    qT = nc.dram_tensor("qT", [6, 128, S], BF16, kind="ExternalOutput").ap()
    vg = nc.dram_tensor("vg", [6, 128, NT, 128], BF16, kind="ExternalOutput").ap()
    cpos = nc.dram_tensor("cpos", [128, NT], F32, kind="ExternalOutput").ap()
    cref = nc.dram_tensor("cref", [128, NT], F32, kind="ExternalOutput").ap()
    with ExitStack() as es:
        s = Sch(nc, es)
        C = _consts(s)
        toks = emit_proj(s, S, x, ng, wc, qkg, bfv, qT, vg, cpos, cref, C)
        s.finish(toks)
        print("proj instrs", s.ninstr)
    return nc


def proj_inputs(x_b, norm_g_l, w_in_l, b_f_l, qk_norm_g_l, h):
    cols = []
    for n, part in ((0, 0), (0, 1), (1, 0), (1, 1), (2, 0), (2, 1)):
        base = n * 4 * 512 + part * 512 + h * 128
        cols.append(np.arange(base, base + 128))
    for n in range(3):
        base = n * 4 * 512 + 2 * 512 + h * 128
        cols.append(np.arange(base, base + 128))
    for n in range(3):
        base = n * 4 * 512 + 3 * 512 + h * 128
        cols.append(np.arange(base, base + 128))
    cols.append(np.array([QKVG + h]))
    cols = np.concatenate(cols)
    wc = np.ascontiguousarray(w_in_l[:, cols])
    qkg = np.ascontiguousarray(qk_norm_g_l.T)
    bfv = np.full((128, 1), b_f_l[h], np.float32)
    return {"x": np.ascontiguousarray(x_b), "ng": np.ascontiguousarray(norm_g_l[None, :]), "wc": wc,
            "qkg": qkg, "bfv": bfv}


def attn_consts_host():
    k = np.arange(128)[:, None]
    q = np.arange(512)[None, :]
    mA = np.stack([(q >= k + 128 * j) for j in range(4)]).astype(np.float32)
    mC = np.stack([(q > k + 128 * j) for j in range(4)]).astype(np.float32)
    mB = []
    for j in range(-4, 4):
        kk = 128 * j + k
        cd = (q // 64) - np.floor_divide(kk, 64)
        mB.append(((cd >= 0) & (cd <= 8)).astype(np.float32))
    mB = np.stack(mB)
    bf = ml_dtypes.bfloat16
    return mA.astype(bf), mB.astype(bf), mC.astype(bf)


def relT_host(rel_bias_h):
    k = np.arange(128)[:, None]
    q = np.arange(512)[None, :]
    out = []
    for j in range(-4, 4):
        diff = q - (128 * j + k)
        idx = np.clip(diff, -128, 128) + 128
        out.append(rel_bias_h[idx])
    return np.ascontiguousarray(np.stack(out).astype(np.float32))


import os as _os
WARM_A = int(_os.environ.get("WARM_A", "0"))
WARM_N = int(_os.environ.get("WARM_N", "256"))


def emit_attn(s, S, qT, vg, cpos_d, cref_d, relT_d, mA_d, mB_d, mC_d, ysT, C, mixers=(0, 1, 2)):
    NT = S // 128
    NB = S // 512
    ident = C["ident"]
    kT = s.sb("kT", [128, S], BF16)
    vS = s.sb("vS", [128, NT, 129], BF16)
    cpos = s.sb("a_cpos", [128, NT], F32)
    cref = s.sb("a_cref", [128, NT], F32)
    relT = s.sb("a_relT", [128, 8, 512], F32)
    mA = s.sb("a_mA", [128, 4, 512], BF16)
    mB = s.sb("a_mB", [128, 8, 512], BF16)
    mC = s.sb("a_mC", [128, 4, 512], BF16)
    onesb = s.sb("a_ones", [128, 128], BF16)
    nu = s.sb("a_nu", [128, 128], BF16)
    negones = s.sb("a_negones", [128, 128], BF16)
    qt = [s.sb(f"a_qt{i}", [128, 512], BF16) for i in range(2)]
    qtd = [s.dsem(f"a_qtd{i}") for i in range(2)]
    sgt = [s.sb(f"a_sg{i}", [128, 4, 128], BF16) for i in range(2)]
    sgd = [s.dsem(f"a_sgd{i}") for i in range(2)]
    bias = [s.sb(f"a_bias{i}", [128, NT], F32) for i in range(2)]
    P = [s.sb(f"a_P{i}", [128, 512], BF16) for i in range(3)]
    L = [s.sb(f"a_L{i}", [128, 512], BF16) for i in range(3)]
    Eb = [s.sb(f"a_E{i}", [128, 512], F32) for i in range(2)]
    tmp = [s.sb(f"a_tmp{i}", [128, 512], F32) for i in range(3)]
    carry = s.sb("a_carry", [128, 512], F32)
    rden = s.sb("a_rden", [128, 4], F32)
    ybf = s.sb("a_ybf", [128, 4, 128], BF16)
    yT = [s.sb(f"a_yT{i}", [128, 512], BF16) for i in range(2)]
    yTd = [s.dsem(f"a_yTd{i}") for i in range(2)]
    psS = [s.ps(f"a_psS{i}", [128, 512], F32) for i in range(2)]
    psR = [s.ps(f"a_psR{i}", [128, 512], F32) for i in range(2)]
    psC = s.ps("a_psC", [128, 512], F32)
    psO = s.ps("a_psO", [128, 2, 512], F32)
    psTt = s.ps("a_psT", [128, 1024], BF16)
    psT = Buf(psTt.ap[:, 0:512], "a_psT_v")
    kd = s.dsem("a_kd")
    vd = s.dsem("a_vd")

    s.dma("sp", cpos[:], cpos_d, s.dsem("a_i0"), writes=[cpos])
    s.dma("sp", cref[:], cref_d, s.dsem("a_i1"), writes=[cref])
    s.dma("sp", relT[:], relT_d.rearrange("j k q -> k j q"), s.dsem("a_i2"), writes=[relT])
    s.dma("sp", mA[:], mA_d.rearrange("j k q -> k j q"), s.dsem("a_i3"), writes=[mA])
    s.dma("sp", mB[:], mB_d.rearrange("j k q -> k j q"), s.dsem("a_i4"), writes=[mB])
    s.dma("sp", mC[:], mC_d.rearrange("j k q -> k j q"), s.dsem("a_i5"), writes=[mC])
    s.op("pool", lambda e: e.memset(onesb[:], 1.0), writes=[onesb])
    s.op("pool", lambda e: e.memset(negones[:], -1.0), writes=[negones])
    s.op("pool", lambda e: e.memset(nu[:], -1.0), writes=[nu])
    s.op("pool", lambda e: e.affine_select(out=nu[:], in_=nu[:], pattern=[[-1, 128]],
                                           compare_op=ALU.is_ge, fill=0.0, base=0, channel_multiplier=1),
         reads=[nu], writes=[nu])
    s.op("pool", lambda e: e.memset(vS[:, :, 128:129], 1.0), writes=[vS])
    if 1 in mixers:
        mneg = s.sb("a_mneg", [128, 8, 512], F32)
        s.op("dve", lambda e: e.tensor_scalar(out=mneg[:], in0=mB[:], scalar1=30000.0, scalar2=-30000.0,
                                              op0=ALU.mult, op1=ALU.add), reads=[mB], writes=[mneg])
        s.op("dve", lambda e: e.tensor_tensor(out=relT[:], in0=relT[:], in1=mB[:], op=ALU.mult),
             reads=[relT, mB], writes=[relT])
        s.op("dve", lambda e: e.tensor_tensor(out=relT[:], in0=relT[:], in1=mneg[:], op=ALU.add),
             reads=[relT, mneg], writes=[relT])

    out_toks = []
    nblk = [0]
    nload = [0]

    def load_kv(m):
        kidx = 2 * m + 1
        s.dma("sp", kT[:], qT[kidx, :, :], kd, writes=[kT])
        for t0 in range(0, NT, 32):
            t1 = min(NT, t0 + 32)
            s.dma("sp", vS[:, t0:t1, 0:128], vg[m, :, t0:t1, :], vd, writes=[vS])

    def load_q(m, qb):
        i = nload[0] % 2
        nload[0] += 1
        s.dma("sp", qt[i][:], qT[2 * m, :, qb * 512:(qb + 1) * 512], qtd[i], writes=[qt[i]])
        s.dma("sp", sgt[i][:], vg[3 + m, :, 4 * qb:4 * qb + 4, :], sgd[i], writes=[sgt[i]])
        return qt[i], sgt[i]

    def po(qs, w):
        return psO[:, qs // 2, (qs % 2) * 256:(qs % 2) * 256 + w]

    def epilogue(m, qb, sg, softmax):
        i = nblk[0] % 2
        if softmax:
            for qs in range(4):
                s.op("dve", lambda e, qs=qs: e.reciprocal(out=rden[:, qs:qs + 1], in_=po(qs, 129)[:, 128:129]),
                     reads=[psO], writes=[rden])
            for qs in range(4):
                s.op("dve", lambda e, qs=qs: e.scalar_tensor_tensor(
                    out=ybf[:, qs, :], in0=po(qs, 128), scalar=rden[:, qs:qs + 1], in1=sg[:, qs, :],
                    op0=ALU.mult, op1=ALU.mult), reads=[psO, rden, sg], writes=[ybf])
        else:
            for qs in range(4):
                s.op("dve", lambda e, qs=qs: e.tensor_tensor(out=ybf[:, qs, :], in0=po(qs, 128),
                                                            in1=sg[:, qs, :], op=ALU.mult),
                     reads=[psO, sg], writes=[ybf])
        s.group("pe", [lambda e, qs=qs: e.transpose(psT[:, qs * 128:(qs + 1) * 128], ybf[:, qs, :], ident[:])
                       for qs in range(4)], reads=[ybf, ident], writes=[psT])
        s.op("dve", lambda e: e.tensor_copy(yT[i][:], psT.ap), reads=[psT], writes=[yT[i]])
        out_toks.append(s.dma("sp", ysT[m, :, qb * 512:(qb + 1) * 512], yT[i][:], yTd[i], reads=[yT[i]]))
        nblk[0] += 1

    def softmax_mixer(m):
        ringS = [psS[0], psS[1], psR[0]]
        load_kv(m)
        tiles = []
        for qb in range(NB):
            kts = list(range(0, 4 * qb + 4)) if m == 0 else list(range(max(0, 4 * qb - 4), 4 * qb + 4))
            for n, kt in enumerate(kts):
                tiles.append((qb, kt, kt - 4 * qb, n == 0, n == len(kts) - 1))
        st = {}

        def stage1(i):
            qb, kt, j, first, last = tiles[i]
            if first:
                st["q"], st["sg"] = load_q(m, qb)
                if m == 0:
                    b = bias[qb % 2]
                    nk = 4 * qb + 4
                    s.op("dve", lambda e: e.tensor_scalar(out=b[:, 0:nk], in0=cpos[:, 0:nk],
                                                          scalar1=cref[:, 4 * qb + 1:4 * qb + 2], scalar2=None,
                                                          op0=ALU.subtract), reads=[cpos, cref], writes=[b])
            q = st["q"]
            tiles[i] = tiles[i] + (q, st["sg"])
            s.op("pe", lambda e: e.matmul(ringS[i % 3][:], kT[:, kt * 128:(kt + 1) * 128], q[:],
                                          start=True, stop=True), reads=[kT, q], writes=[ringS[i % 3]])
            if WARM_A:
                s.group("pe", [lambda e: e.matmul(psR[0][:, 0:WARM_N], kT[:, kt * 128:(kt + 1) * 128], q[:, 0:WARM_N],
                                                  start=True, stop=True) for _ in range(WARM_A)],
                        reads=[kT, q], writes=[psR[0]])

        def stage2(i):
            qb, kt, j, first, last, q, sg = tiles[i]
            p = P[i % 3]
            if m == 0:
                b = bias[qb % 2]
                s.op("act", lambda e: e.activation(out=p[:], in_=ringS[i % 3][:], func=AF.Exp,
                                                   bias=b[:, kt:kt + 1]), reads=[ringS[i % 3], b], writes=[p])
                if j >= 0:
                    s.op("pool", lambda e: e.tensor_tensor(out=p[:], in0=p[:], in1=mA[:, j, :], op=ALU.mult),
                         reads=[p, mA], writes=[p])
            else:
                t = tmp[i % 3]
                s.op("dve", lambda e: e.tensor_tensor(out=t[:], in0=ringS[i % 3][:], in1=relT[:, j + 4, :],
                                                      op=ALU.add), reads=[ringS[i % 3], relT], writes=[t])
                s.op("act", lambda e: e.activation(out=p[:], in_=t[:], func=AF.Exp), reads=[t], writes=[p])
            s.group("pe", [lambda e, qs=qs: e.matmul(po(qs, 129), p[:, qs * 128:(qs + 1) * 128], vS[:, kt, :],
                                                     start=(first and qs % 2 == 0), stop=last,
                                                     skip_group_check=True) for qs in range(4)],
                    reads=[p, vS], writes=[psO])
            if last:
                epilogue(m, qb, sg, True)

        n = len(tiles)
        for i in range(min(2, n)):
            stage1(i)
        for i in range(n):
            if i + 2 < n:
                stage1(i + 2)
            stage2(i)

    def sb_mixer():
        m = 2
        load_kv(m)
        tiles = []
        for qb in range(NB):
            kts = list(range(4 * qb + 3, -1, -1))
            for n, kt in enumerate(kts):
                tiles.append((qb, kt, kt - 4 * qb, n == 0, n == len(kts) - 1))
        st = {}

        def stZ(i):
            qb, kt, j, first, last = tiles[i]
            if first:
                st["q"], st["sg"] = load_q(m, qb)
            q = st["q"]
            tiles[i] = tiles[i] + (q, st["sg"])
            fns = [lambda e: e.matmul(psT.ap, ident[:], kT[:, kt * 128:kt * 128 + 512 if kt * 128 + 512 <= S else kt * 128],
                                      start=True, stop=True) for _ in range(0)]
            fns.append(lambda e: e.matmul(psS[i % 2][:], kT[:, kt * 128:(kt + 1) * 128], q[:],
                                          start=True, stop=True))
            s.group("pe", fns, reads=[kT, q], writes=[psS[i % 2]])

        def stEL(i):
            qb, kt, j, first, last, q, sg = tiles[i]
            E = Eb[i % 2]
            l = L[i % 3]
            s.op("act", lambda e: e.activation(out=E[:], in_=psS[i % 2][:], func=AF.Exp),
                 reads=[psS[i % 2]], writes=[E])
            s.op("act", lambda e: e.activation(out=l[:], in_=E[:], func=AF.Ln, bias=1.0), reads=[E], writes=[l])
            if j >= 0:
                s.op("pool", lambda e: e.tensor_tensor(out=l[:], in0=l[:], in1=mC[:, j, :], op=ALU.mult),
                     reads=[l, mC], writes=[l])

        def stR(i):
            qb, kt, j, first, last, q, sg = tiles[i]
            l = L[i % 3]
            r = psR[i % 2]
            s.group("pe", [lambda e: e.matmul(r[:], nu[:], l[:], start=True, stop=False),
                           lambda e: e.matmul(r[:], kT[:, kt * 128:(kt + 1) * 128], q[:], start=False, stop=True)],
                    reads=[nu, l, kT, q], writes=[r])
            t = tmp[i % 3]
            if first:
                s.op("dve", lambda e: e.tensor_copy(t[:], r[:]), reads=[r], writes=[t])
            else:
                s.op("dve", lambda e: e.tensor_tensor(out=t[:], in0=r[:], in1=carry[:], op=ALU.add),
                     reads=[r, carry], writes=[t])
            if not last:
                s.op("pe", lambda e: e.matmul(psC[:], negones[:], l[:], start=True, stop=True),
                     reads=[negones, l], writes=[psC])
                if first:
                    s.op("dve", lambda e: e.tensor_copy(carry[:], psC[:]), reads=[psC], writes=[carry])
                else:
                    s.op("dve", lambda e: e.tensor_tensor(out=carry[:], in0=psC[:], in1=carry[:], op=ALU.add),
                         reads=[psC, carry], writes=[carry])

        def stA(i):
            qb, kt, j, first, last, q, sg = tiles[i]
            p = P[i % 3]
            t = tmp[i % 3]
            s.op("act", lambda e: e.activation(out=p[:], in_=t[:], func=AF.Exp), reads=[t], writes=[p])
            if j >= 0:
                s.op("pool", lambda e: e.tensor_tensor(out=p[:], in0=p[:], in1=mC[:, j, :], op=ALU.mult),
                     reads=[p, mC], writes=[p])
            s.group("pe", [lambda e, qs=qs: e.matmul(po(qs, 128), p[:, qs * 128:(qs + 1) * 128], vS[:, kt, 0:128],
                                                     start=(first and qs % 2 == 0), stop=last,
                                                     skip_group_check=True) for qs in range(4)],
                    reads=[p, vS], writes=[psO])
            if last:
                epilogue(m, qb, sg, False)

        n = len(tiles)
        stZ(0)
        for i in range(n + 2):
            if i + 1 < n:
                stZ(i + 1)
            if i < n:
                stEL(i)
            if 1 <= i <= n:
                stR(i - 1)
            if i >= 2:
                stA(i - 2)

    for m in mixers:
        if m in (0, 1):
            softmax_mixer(m)
        else:
            sb_mixer()
    return out_toks


def build_attn(S, mixers=(0, 1, 2)):
    nc = bass.Bass("TRN2", target_bir_lowering=False)
    NT = S // 128
    qT = nc.dram_tensor("qT", [6, 128, S], BF16, kind="ExternalInput").ap()
    vg = nc.dram_tensor("vg", [6, 128, NT, 128], BF16, kind="ExternalInput").ap()
    cpos = nc.dram_tensor("cpos", [128, NT], F32, kind="ExternalInput").ap()
    cref = nc.dram_tensor("cref", [128, NT], F32, kind="ExternalInput").ap()
    relT = nc.dram_tensor("relT", [8, 128, 512], F32, kind="ExternalInput").ap()
    mA = nc.dram_tensor("mA", [4, 128, 512], BF16, kind="ExternalInput").ap()
    mB = nc.dram_tensor("mB", [8, 128, 512], BF16, kind="ExternalInput").ap()
    mC = nc.dram_tensor("mC", [4, 128, 512], BF16, kind="ExternalInput").ap()
    ysT = nc.dram_tensor("ysT", [3, 128, S], BF16, kind="ExternalOutput").ap()
    with ExitStack() as es:
        s = Sch(nc, es)
        C = _consts(s)
        toks = emit_attn(s, S, qT, vg, cpos, cref, relT, mA, mB, mC, ysT, C, mixers)
        s.finish(toks)
        print("attn instrs", s.ninstr)
    return nc


def emit_merge(s, T, x, ng, wg, bg, wup, wout, ysT, y, C, TB=1024):
    ident = C["ident"]
    NBLK = T // TB
    NTT = TB // 128
    NTG = TB // 512
    gB = s.sb("m_gB", [128, D], F32)
    bgs = s.sb("m_bg", [128, 48], F32)
    xt = [s.sb(f"m_xt{i}", [128, D], F32) for i in range(2)]
    xds = [s.dsem(f"m_xds{i}") for i in range(2)]
    junk = s.sb("m_junk", [128, D], BF16)
    ss = s.sb("m_ss", [128, 4], F32)
    hn = [s.sb(f"m_hn{i}", [128, D], BF16) for i in range(2)]
    hnT = s.sb("m_hnT", [128, NCH, TB], BF16)
    ys = s.sb("m_ys", [128, 12, TB], BF16)
    ysd = s.dsem("m_ysd")
    mT = s.sb("m_mT", [128, NCH, TB], BF16)
    NWB = 4
    wgc = [s.sb(f"m_wgc{i}", [128, NCH, 128], BF16) for i in range(NWB)]
    wgd = [s.dsem(f"m_wgd{i}") for i in range(NWB)]
    wuc = [s.sb(f"m_wuc{i}", [128, 4, 128], BF16) for i in range(NWB)]
    wud = [s.dsem(f"m_wud{i}") for i in range(NWB)]
    woc = [s.sb(f"m_woc{i}", [128, NCH, 512], BF16) for i in range(2)]
    wod = [s.dsem(f"m_wod{i}") for i in range(2)]
    sig = [s.sb(f"m_sig{i}", [128, 512], F32) for i in range(2)]
    prod = [s.sb(f"m_prod{i}", [128, 512], F32) for i in range(2)]
    acc = [s.sb(f"m_acc{i}", [128, 512], F32) for i in range(NTG)]
    xo = [s.sb(f"m_xo{i}", [128, 512], F32) for i in range(3)]
    xod = [s.dsem(f"m_xod{i}") for i in range(3)]
    xsd = [s.dsem(f"m_xsd{i}") for i in range(3)]
    psT = s.ps("m_psT", [128, D], BF16)
    psG = [s.ps(f"m_psG{i}", [128, 512], F32) for i in range(2)]
    psU = [s.ps(f"m_psU{i}", [128, 512], F32) for i in range(2)]
    psY = [s.ps(f"m_psY{i}", [128, 512], F32) for i in range(2)]

    s.dma("sp", gB[:], ng.partition_broadcast(128), s.dsem("m_i0"), writes=[gB])
    s.dma("sp", bgs[:], bg, s.dsem("m_i1"), writes=[bgs])
    out_toks = []
    nw = 0
    nwi = 0
    nwo = 0
    nx = 0
    nxo = 0
    for blk in range(NBLK):
        t0 = blk * TB
        for n in range(3):
            s.dma("sp", ys[:, n * 4:(n + 1) * 4, :], ysT[n, :, :, t0:t0 + TB].rearrange("h p t -> p h t"),
                  ysd, writes=[ys])
        for tt in range(NTT):
            xs = xt[nx % 2]
            s.dma("sp", xs[:], x[t0 + tt * 128:t0 + (tt + 1) * 128, :], xds[nx % 2], writes=[xs])
            s.op("dve", lambda e: e.memset(ss[:, 0:1], 0.0), writes=[ss])
            s.op("act", lambda e: e.activation(out=junk[:], in_=xs[:], func=AF.Square, accum_out=ss[:, 0:1]),
                 reads=[xs], writes=[junk, ss])
            s.op("act", lambda e: e.activation(out=ss[:, 1:2], in_=ss[:, 0:1], func=AF.Ln,
                                               scale=1.0 / D, bias=EPS), reads=[ss], writes=[ss])
            s.op("act", lambda e: e.activation(out=ss[:, 2:3], in_=ss[:, 1:2], func=AF.Exp, scale=-0.5),
                 reads=[ss], writes=[ss])
            h = hn[nx % 2]
            s.op("dve", lambda e: e.scalar_tensor_tensor(out=h[:], in0=xs[:], scalar=ss[:, 2:3], in1=gB[:],
                                                         op0=ALU.mult, op1=ALU.mult),
                 reads=[xs, ss, gB], writes=[h])
            s.group("pe", [lambda e, c=c: e.transpose(psT[:, c * 128:(c + 1) * 128],
                                                      h[:, c * 128:(c + 1) * 128], ident[:])
                           for c in range(NCH)], reads=[h, ident], writes=[psT])
            pv = psT.ap[:].rearrange("p (c t) -> p c t", c=NCH)
            s.op("act", lambda e: e.copy(out=hnT[:, 0:8, tt * 128:(tt + 1) * 128], in_=pv[:, 0:8, :]),
                 reads=[psT], writes=[hnT])
            s.op("dve", lambda e: e.tensor_copy(hnT[:, 8:16, tt * 128:(tt + 1) * 128], pv[:, 8:16, :]),
                 reads=[psT], writes=[hnT])
            nx += 1
        for dc in range(NCH):
            for n in range(3):
                while nwi < min(nw + NWB, NBLK * 48):
                    ci = nwi % 48
                    s.dma("pool", wgc[nwi % NWB][:], wg[ci], wgd[nwi % NWB], writes=[wgc[nwi % NWB]])
                    s.dma("pool", wuc[nwi % NWB][:], wup[ci], wud[nwi % NWB], writes=[wuc[nwi % NWB]])
                    nwi += 1
                wgb = wgc[nw % NWB]
                wub = wuc[nw % NWB]
                nw += 1
                for tg in range(NTG):
                    pg = psG[tg % 2]
                    pu = psU[tg % 2]
                    s.group("pe", [lambda e, c=c: e.matmul(pg[:], wgb[:, c, :], hnT[:, c, tg * 512:(tg + 1) * 512],
                                                           start=(c == 0), stop=(c == NCH - 1))
                                   for c in range(NCH)], reads=[wgb, hnT], writes=[pg])
                    s.group("pe", [lambda e, k=k: e.matmul(pu[:], wub[:, k, :],
                                                           ys[:, n * 4 + k, tg * 512:(tg + 1) * 512],
                                                           start=(k == 0), stop=(k == 3))
                                   for k in range(4)], reads=[wub, ys], writes=[pu])
                    sg = sig[tg % 2]
                    s.op("act", lambda e: e.activation(out=sg[:], in_=pg[:], func=AF.Sigmoid,
                                                       bias=bgs[:, n * 16 + dc:n * 16 + dc + 1]),
                         reads=[pg, bgs], writes=[sg])
                    a = acc[tg]
                    if n == 0:
                        s.op("dve", lambda e: e.tensor_tensor(out=a[:], in0=pu[:], in1=sg[:], op=ALU.mult),
                             reads=[pu, sg], writes=[a])
                    else:
                        pr = prod[tg % 2]
                        s.op("dve", lambda e: e.tensor_tensor(out=pr[:], in0=pu[:], in1=sg[:], op=ALU.mult),
                             reads=[pu, sg], writes=[pr])
                        if n == 1:
                            s.op("dve", lambda e: e.tensor_tensor(out=a[:], in0=a[:], in1=pr[:], op=ALU.add),
                                 reads=[a, pr], writes=[a])
                        else:
                            s.op("dve", lambda e: e.tensor_tensor(out=mT[:, dc, tg * 512:(tg + 1) * 512],
                                                                  in0=a[:], in1=pr[:], op=ALU.add),
                                 reads=[a, pr], writes=[mT])
        for cg in range(4):
            wo = woc[nwo % 2]
            s.dma("pool", wo[:], wout[cg], wod[nwo % 2], writes=[wo])
            nwo += 1
            for tt in range(NTT):
                xb = xo[nxo % 3]
                rows = slice(t0 + tt * 128, t0 + (tt + 1) * 128)
                s.dma("sp", xb[:], x[rows, cg * 512:(cg + 1) * 512], xod[nxo % 3], writes=[xb])
                py = psY[nxo % 2]
                s.group("pe", [lambda e, c=c: e.matmul(py[:], mT[:, c, tt * 128:(tt + 1) * 128], wo[:, c, :],
                                                       start=(c == 0), stop=(c == NCH - 1))
                               for c in range(NCH)], reads=[mT, wo], writes=[py])
                s.op("dve", lambda e: e.tensor_tensor(out=xb[:], in0=py[:], in1=xb[:], op=ALU.add),
                     reads=[py, xb], writes=[xb])
                out_toks.append(s.dma("sp", y[rows, cg * 512:(cg + 1) * 512], xb[:], xsd[nxo % 3], reads=[xb]))
                nxo += 1
    return out_toks


def build_merge(T, TB=1024):
    nc = bass.Bass("TRN2", target_bir_lowering=False)
    x = nc.dram_tensor("x", [T, D], F32, kind="ExternalInput").ap()
    ng = nc.dram_tensor("ng", [1, D], F32, kind="ExternalInput").ap()
    wg = nc.dram_tensor("wg", [48, 128, NCH, 128], F32, kind="ExternalInput").ap()
    bg = nc.dram_tensor("bg", [128, 48], F32, kind="ExternalInput").ap()
    wup = nc.dram_tensor("wup", [48, 128, 4, 128], F32, kind="ExternalInput").ap()
    wout = nc.dram_tensor("wout", [4, 128, NCH, 512], F32, kind="ExternalInput").ap()
    ysT = nc.dram_tensor("ysT", [3, 4, 128, T], BF16, kind="ExternalInput").ap()
    y = nc.dram_tensor("y", [T, D], F32, kind="ExternalOutput").ap()
    with ExitStack() as es:
        s = Sch(nc, es)
        C = _consts(s)
        toks = emit_merge(s, T, x, ng, wg, bg, wup, wout, ysT, y, C, TB)
        s.finish(toks)
        print("merge instrs", s.ninstr)
    return nc


def build_pa(S):
    nc = bass.Bass("TRN2", target_bir_lowering=False)
    NT = S // 128
    x = nc.dram_tensor("x", [S, D], F32, kind="ExternalInput").ap()
    ng = nc.dram_tensor("ng", [1, D], F32, kind="ExternalInput").ap()
    wc = nc.dram_tensor("wc", [D, WC_COLS], F32, kind="ExternalInput").ap()
    qkg = nc.dram_tensor("qkg", [128, 4], F32, kind="ExternalInput").ap()
    bfv = nc.dram_tensor("bfv", [128, 1], F32, kind="ExternalInput").ap()
    relT = nc.dram_tensor("relT", [8, 128, 512], F32, kind="ExternalInput").ap()
    mA = nc.dram_tensor("mA", [4, 128, 512], BF16, kind="ExternalInput").ap()
    mB = nc.dram_tensor("mB", [8, 128, 512], BF16, kind="ExternalInput").ap()
    mC = nc.dram_tensor("mC", [4, 128, 512], BF16, kind="ExternalInput").ap()
    ysT = nc.dram_tensor("ysT", [3, 128, S], BF16, kind="ExternalOutput").ap()
    qT = nc.dram_tensor("qT_i", [6, 128, S], BF16, kind="Internal").ap()
    vg = nc.dram_tensor("vg_i", [6, 128, NT, 128], BF16, kind="Internal").ap()
    cpos = nc.dram_tensor("cpos_i", [128, NT], F32, kind="Internal").ap()
    cref = nc.dram_tensor("cref_i", [128, NT], F32, kind="Internal").ap()
    with ExitStack() as es:
        s = Sch(nc, es)
        C = _consts(s)
        with s.scope():
            toks = emit_proj(s, S, x, ng, wc, qkg, bfv, qT, vg, cpos, cref, C)
            s.barrier(toks)
        with s.scope():
            toks = emit_attn(s, S, qT, vg, cpos, cref, relT, mA, mB, mC, ysT, C)
            s.finish(toks)
        print("pa instrs", s.ninstr)
    return nc


def merge_weights_host(w_in_l, w_up_l, w_out_l):
    wg = w_in_l[:, QKVG + 4:].reshape(NCH, 128, 3, NCH, 128).transpose(3, 2, 1, 0, 4)
    wg = np.ascontiguousarray(wg).reshape(48, 128, NCH, 128)
    wup = w_up_l.reshape(3, 4, 128, NCH, 128).transpose(3, 0, 2, 1, 4)
    wup = np.ascontiguousarray(wup).reshape(48, 128, 4, 128)
    wout = np.ascontiguousarray(w_out_l.reshape(NCH, 128, 4, 512).transpose(2, 1, 0, 3))
    return wg, wup, wout


BATCH = 2
SEQ = 16384
DEPTH = 2
_PROGS = {}


def _prog(name, fn, *a):
    key = (name,) + a
    if key not in _PROGS:
        _PROGS[key] = fn(*a)
    return _PROGS[key]


def kernel(x, norm_g, w_in, b_f, b_gate, qk_norm_g, rel_bias, w_up, w_out):
    x = np.asarray(x, np.float32)
    norm_g = np.asarray(norm_g, np.float32)
    w_in = np.asarray(w_in, np.float32)
    b_f = np.asarray(b_f, np.float32)
    b_gate = np.asarray(b_gate, np.float32)
    qk_norm_g = np.asarray(qk_norm_g, np.float32)
    rel_bias = np.asarray(rel_bias, np.float32)
    w_up = np.asarray(w_up, np.float32)
    w_out = np.asarray(w_out, np.float32)
    cores = list(range(8))
    mA, mB, mC = attn_consts_host()
    TQ = SEQ // 4
    cur = x
    for l in range(DEPTH):
        ncpa = _prog("pa", build_pa, SEQ)
        ims = []
        for c in cores:
            im = proj_inputs(cur[c // 4], norm_g[l], w_in[l], b_f[l], qk_norm_g[l], c % 4)
            im.update({"relT": relT_host(rel_bias[l, c % 4]), "mA": mA, "mB": mB, "mC": mC})
            ims.append(im)
        ra = run_bass_kernel_spmd(ncpa, ims, core_ids=cores).results
        del ims
        ncm = _prog("merge", build_merge, TQ)
        wg, wup_h, wout_h = merge_weights_host(w_in[l], w_up[l], w_out[l])
        bg = np.ascontiguousarray(b_gate[l].reshape(48, 128).T)
        ims = []
        for c in cores:
            b, tq = c // 4, c % 4
            ysT = np.ascontiguousarray(np.stack([ra[b * 4 + h]["ysT"][:, :, tq * TQ:(tq + 1) * TQ]
                                                 for h in range(4)], axis=1))
            ims.append({"x": np.ascontiguousarray(cur[b, tq * TQ:(tq + 1) * TQ]), "ng": norm_g[l][None, :],
                        "wg": wg, "bg": bg, "wup": wup_h, "wout": wout_h, "ysT": ysT})
        rm = run_bass_kernel_spmd(ncm, ims, core_ids=cores).results
        del ims, ra
        nxt = np.empty_like(cur)
        for c in cores:
            b, tq = c // 4, c % 4
            nxt[b, tq * TQ:(tq + 1) * TQ] = rm[c]["y"]
        cur = nxt
    return cur
```

```python
import numpy as np
from contextlib import ExitStack
import ml_dtypes
import concourse.bass as bass
import concourse.mybir as mybir
from concourse.bass_utils import run_bass_kernel_spmd

F32 = mybir.dt.float32
BF16 = mybir.dt.bfloat16
AF = mybir.ActivationFunctionType
ALU = mybir.AluOpType
AX = mybir.AxisListType

D = 2048
NCH = 16
HD = 128
EPS = 1e-6
SCALE = HD ** -0.5
QKVG = 6144
N_HEADS = 4
FM_COLS = 768
TM_COLS = 769
WC_COLS = FM_COLS + TM_COLS


class Buf:
    __slots__ = ("ap", "w", "r", "name")

    def __init__(self, ap, name=""):
        self.ap = ap
        self.w = None
        self.r = []
        self.name = name

    def __getitem__(self, idx):
        return self.ap[idx]


class DSem:
    def __init__(self, sem, name):
        self.sem = sem
        self.n = 0
        self.name = name


class Sch:
    def __init__(self, nc, es):
        self.nc = nc
        self.es = es
        self.eng = {"pe": nc.tensor, "act": nc.scalar, "dve": nc.vector, "pool": nc.gpsimd,
                    "sp": nc.sync}
        self.sem = {k: es.enter_context(nc.semaphore("prog_" + k)) for k in self.eng}
        self.cnt = {k: 0 for k in self.eng}
        self.seen = {k: {} for k in self.eng}
        self.nd = 0
        self.ninstr = 0

    def dsem(self, name=None):
        self.nd += 1
        name = name or f"dma{self.nd}"
        return DSem(self.es.enter_context(self.nc.semaphore(name)), name)

    def sb(self, name, shape, dt):
        return Buf(self.es.enter_context(self.nc.sbuf_tensor(name, shape, dt)), name)

    def ps(self, name, shape, dt):
        return Buf(self.es.enter_context(self.nc.psum_tensor(name, shape, dt)), name)

    def _wait(self, e, toks):
        seen = self.seen[e]
        for t in toks:
            if t is None:
                continue
            key, sem, val = t
            if key == "pe" and e == "pe":
                continue
            if seen.get(key, 0) >= val:
                continue
            self.eng[e].wait_ge(sem, val)
            seen[key] = val

    def _deps(self, reads, writes):
        deps = []
        for b in reads:
            deps.append(b.w)
        for b in writes:
            deps.append(b.w)
            deps.extend(b.r)
        return deps

    def _commit(self, tok, reads, writes):
        for b in writes:
            b.w = tok
            b.r = []
        for b in reads:
            if b not in writes:
                b.r.append(tok)
                if len(b.r) > 24:
                    b.r = b.r[-24:]

    def op(self, e, fn, reads=(), writes=(), extra=()):
        self._wait(e, self._deps(reads, writes) + list(extra))
        ins = fn(self.eng[e])
        self.cnt[e] += 1
        ins.then_inc(self.sem[e], 1)
        self.ninstr += 1
        tok = (e, self.sem[e], self.cnt[e])
        self._commit(tok, reads, writes)
        return tok

    def group(self, e, fns, reads=(), writes=(), extra=()):
        self._wait(e, self._deps(reads, writes) + list(extra))
        ins = None
        for fn in fns:
            ins = fn(self.eng[e])
            self.ninstr += 1
        self.cnt[e] += 1
        ins.then_inc(self.sem[e], 1)
        tok = (e, self.sem[e], self.cnt[e])
        self._commit(tok, reads, writes)
        return tok

    def dma(self, e, out, in_, ds, reads=(), writes=(), extra=()):
        self._wait(e, self._deps(reads, writes) + list(extra))
        ins = self.eng[e].dma_start(out=out, in_=in_)
        ds.n += 16
        ins.then_inc(ds.sem, 16)
        self.ninstr += 1
        tok = (ds.name, ds.sem, ds.n)
        self._commit(tok, reads, writes)
        return tok

    def finish(self, toks):
        self._wait("sp", toks)

    def barrier(self, toks=()):
        cur = [(k, self.sem[k], self.cnt[k]) for k in self.eng if self.cnt[k] > 0]
        for e in self.eng:
            self._wait(e, [t for t in cur if t[0] != e] + list(toks))

    def scope(self):
        outer = self

        class _Scope:
            def __enter__(self_inner):
                self_inner.saved = outer.es
                self_inner.es = ExitStack()
                self_inner.es.__enter__()
                outer.es = self_inner.es
                return outer

            def __exit__(self_inner, *a):
                outer.es = self_inner.saved
                return self_inner.es.__exit__(*a)
        return _Scope()


def _consts(s):
    c = {}
    c["ident"] = s.sb("ident", [128, 128], BF16)
    s.op("pool", lambda e: e.memset(c["ident"][:], 1.0), writes=[c["ident"]])
    s.op("pool", lambda e: e.affine_select(out=c["ident"][:], in_=c["ident"][:], pattern=[[-1, 128]],
                                           compare_op=ALU.is_equal, fill=0.0, base=0,
                                           channel_multiplier=1),
         reads=[c["ident"]], writes=[c["ident"]])
    return c


def emit_proj(s, S, x, ng, wc, qkg, bfv, qT, vg, cpos_out, cref_out, C):
    nc = s.nc
    NT = S // 128
    NG = S // 512
    ident = C["ident"]
    W = s.sb("W", [128, NCH, WC_COLS], BF16)
    gB = s.sb("gB", [128, D], F32)
    gcol = s.sb("gcol", [128, 6], F32)
    nbf = s.sb("nbf", [128, 1], F32)
    ones_inv = s.sb("ones_inv", [128, 128], BF16)
    xt = [s.sb(f"xt{i}", [128, D], F32) for i in range(3)]
    xds = [s.dsem(f"xds{i}") for i in range(3)]
    junk = s.sb("junk", [128, D], BF16)
    ss = s.sb("ss", [128, 4], F32)
    hn = [s.sb(f"hn{i}", [128, D], BF16) for i in range(2)]
    hnT = [s.sb(f"hnT{i}", [128, NCH, 512], BF16) for i in range(2)]
    vgt = [s.sb(f"vgt{i}", [128, 768], BF16) for i in range(2)]
    vds = [s.dsem(f"vds{i}") for i in range(2)]
    ebuf = s.sb("ebuf", [128, 384], F32)
    ef = s.sb("ef", [128, 2], F32)
    lfcol = s.sb("lfcol", [128, 128], F32)
    sq = s.sb("sq", [128, 512], BF16)
    lnv = s.sb("lnv", [128, 512], F32)
    rstd = s.sb("rstdb", [128, 512], F32)
    qTt = [s.sb(f"qTt{i}", [128, 6, 512], BF16) for i in range(2)]
    qds = [s.dsem(f"qds{i}") for i in range(2)]
    psTa = s.ps("psTa", [128, 1024], BF16)
    psTb = s.ps("psTb", [128, 1024], BF16)
    ss2 = [s.sb(f"ss2_{i}", [128, 4], F32) for i in range(2)]
    ebuf2 = [s.sb(f"ebuf2_{i}", [128, 384], F32) for i in range(2)]
    psA = s.ps("psA", [128, 512], F32)
    psB = s.ps("psB", [128, 512], F32)
    psF = [s.ps(f"psF{i}", [128, 512], F32) for i in range(2)]
    psM = s.ps("psM", [128, 512], F32)
    ids = s.dsem("init")

    for c in range(NCH):
        s.dma("pool", W[:, c, :], wc[c * 128:(c + 1) * 128, :], ids, writes=[W])
    s.dma("sp", gB[:], ng.partition_broadcast(128), s.dsem("i_gB"), writes=[gB])
    s.dma("sp", gcol[:, 0:4], qkg, s.dsem("i_gcol"), writes=[gcol])
    s.dma("sp", nbf[:], bfv, s.dsem("i_nbf"), writes=[nbf])
    s.op("pool", lambda e: e.memset(ones_inv[:], 1.0 / 128.0), writes=[ones_inv])
    s.op("pool", lambda e: e.memset(lfcol[:], 0.0), writes=[lfcol])
    s.op("pool", lambda e: e.memset(gcol[:, 4:6], 1.0), reads=[], writes=[gcol])
    for j in (0, 2, 4):
        s.op("dve", lambda e, j=j: e.tensor_scalar(out=gcol[:, j:j + 1], in0=gcol[:, j:j + 1],
                                                    scalar1=SCALE, scalar2=None, op0=ALU.mult),
             reads=[gcol], writes=[gcol])
    s.op("dve", lambda e: e.tensor_scalar(out=nbf[:], in0=nbf[:], scalar1=-1.0, scalar2=None,
                                          op0=ALU.mult), reads=[nbf], writes=[nbf])

    out_toks = []
    pvA = psTa.ap[:].rearrange("p (c t) -> p c t", c=8)
    pvB = psTb.ap[:].rearrange("p (c t) -> p c t", c=8)

    def stage_load(ti):
        s.dma("sp", xt[ti % 3][:], x[ti * 128:(ti + 1) * 128, :], xds[ti % 3], writes=[xt[ti % 3]])

    def stage_norm(ti):
        g, tt = divmod(ti, 4)
        hT = hnT[g % 2]
        xs = xt[ti % 3]
        sst = ss2[ti % 2]
        s.op("dve", lambda e: e.memset(sst[:, 0:1], 0.0), writes=[sst])
        s.op("act", lambda e: e.activation(out=junk[:], in_=xs[:], func=AF.Square, accum_out=sst[:, 0:1]),
             reads=[xs], writes=[junk, sst])
        s.op("act", lambda e: e.activation(out=sst[:, 1:2], in_=sst[:, 0:1], func=AF.Ln,
                                           scale=1.0 / D, bias=EPS), reads=[sst], writes=[sst])
        s.op("act", lambda e: e.activation(out=sst[:, 2:3], in_=sst[:, 1:2], func=AF.Exp, scale=-0.5),
             reads=[sst], writes=[sst])
        h = hn[ti % 2]
        s.op("dve", lambda e: e.scalar_tensor_tensor(out=h[:], in0=xs[:], scalar=sst[:, 2:3], in1=gB[:],
                                                     op0=ALU.mult, op1=ALU.mult),
             reads=[xs, sst, gB], writes=[h])

    def stage_tr(ti):
        g, tt = divmod(ti, 4)
        hT = hnT[g % 2]
        h = hn[ti % 2]
        s.group("pe", [lambda e, c=c: e.transpose(psTa[:, c * 128:(c + 1) * 128],
                                                  h[:, c * 128:(c + 1) * 128], ident[:])
                       for c in range(8)], reads=[h, ident], writes=[psTa])
        s.group("pe", [lambda e, c=c: e.transpose(psTb[:, c * 128:(c + 1) * 128],
                                                  h[:, (8 + c) * 128:(9 + c) * 128], ident[:])
                       for c in range(8)], reads=[h, ident], writes=[psTb])
        s.op("act", lambda e: e.copy(out=hT[:, 0:8, tt * 128:(tt + 1) * 128], in_=pvA),
             reads=[psTa], writes=[hT])
        s.op("dve", lambda e: e.tensor_copy(hT[:, 8:16, tt * 128:(tt + 1) * 128], pvB),
             reads=[psTb], writes=[hT])

    def stage_tm(ti):
        g, tt = divmod(ti, 4)
        hT = hnT[g % 2]
        fns = []
        for c in range(NCH):
            fns.append(lambda e, c=c: e.matmul(psA[:], hT[:, c, tt * 128:(tt + 1) * 128],
                                               W[:, c, 768:1280], start=(c == 0), stop=(c == NCH - 1)))
        for c in range(NCH):
            fns.append(lambda e, c=c: e.matmul(psB[:, 0:257], hT[:, c, tt * 128:(tt + 1) * 128],
                                               W[:, c, 1280:1537], start=(c == 0), stop=(c == NCH - 1)))
        s.group("pe", fns, reads=[hT, W], writes=[psA, psB])
        vt = vgt[ti % 2]
        eb = ebuf2[ti % 2]
        s.op("act", lambda e: e.copy(out=vt[:, 0:384], in_=psA[:, 0:384]), reads=[psA], writes=[vt])
        s.op("act", lambda e: e.activation(out=eb[:, 0:128], in_=psA[:, 384:512], func=AF.Exp,
                                           scale=-1.0), reads=[psA], writes=[eb])
        s.op("act", lambda e: e.activation(out=eb[:, 128:384], in_=psB[:, 0:256], func=AF.Exp,
                                           scale=-1.0), reads=[psB], writes=[eb])
        s.op("act", lambda e: e.activation(out=ef[:, 0:1], in_=psB[:, 256:257], func=AF.Exp,
                                           scale=-1.0, bias=nbf[:, 0:1]), reads=[psB, nbf], writes=[ef])
        s.op("act", lambda e: e.activation(out=lfcol[:, ti:ti + 1], in_=ef[:, 0:1], func=AF.Ln,
                                           bias=1.0), reads=[ef], writes=[lfcol])
        s.op("act", lambda e: e.activation(out=eb[:], in_=eb[:], func=AF.Ln, bias=1.0), reads=[eb], writes=[eb])
        s.op("act", lambda e: e.activation(out=eb[:], in_=eb[:], func=AF.Exp, scale=-1.0), reads=[eb], writes=[eb])
        s.op("dve", lambda e: e.tensor_tensor(out=vt[:, 384:512], in0=psA[:, 384:512],
                                              in1=eb[:, 0:128], op=ALU.mult),
             reads=[psA, eb], writes=[vt])
        s.op("dve", lambda e: e.tensor_tensor(out=vt[:, 512:768], in0=psB[:, 0:256],
                                              in1=eb[:, 128:384], op=ALU.mult),
             reads=[psB, eb], writes=[vt])
        out_toks.append(s.dma("pool", vg[:, :, ti, :].rearrange("m p d -> p m d"),
                              vt.ap[:].rearrange("p (m d) -> p m d", m=6), vds[ti % 2], reads=[vt]))

    def stage_fm(g, chunks):
        hT = hnT[g % 2]
        qt = qTt[g % 2]
        for cc in chunks:
            pf = psF[cc % 2]
            s.group("pe", [lambda e, c=c: e.matmul(pf[:], W[:, c, cc * 128:(cc + 1) * 128], hT[:, c, :],
                                                   start=(c == 0), stop=(c == NCH - 1))
                           for c in range(NCH)], reads=[W, hT], writes=[pf])
            if cc < 4:
                s.op("act", lambda e: e.activation(out=sq[:], in_=pf[:], func=AF.Square),
                     reads=[pf], writes=[sq])
                s.op("pe", lambda e: e.matmul(psM[:], ones_inv[:], sq[:], start=True, stop=True),
                     reads=[ones_inv, sq], writes=[psM])
                s.op("act", lambda e: e.activation(out=lnv[:], in_=psM[:], func=AF.Ln, bias=EPS),
                     reads=[psM], writes=[lnv])
                s.op("act", lambda e: e.activation(out=rstd[:], in_=lnv[:], func=AF.Exp, scale=-0.5),
                     reads=[lnv], writes=[rstd])
                s.op("dve", lambda e: e.scalar_tensor_tensor(out=qt[:, cc, :], in0=pf[:],
                                                             scalar=gcol[:, cc:cc + 1], in1=rstd[:],
                                                             op0=ALU.mult, op1=ALU.mult),
                     reads=[pf, gcol, rstd], writes=[qt])
            else:
                s.op("dve", lambda e: e.tensor_scalar(out=qt[:, cc, :], in0=pf[:],
                                                      scalar1=gcol[:, cc:cc + 1], scalar2=None,
                                                      op0=ALU.mult), reads=[pf, gcol], writes=[qt])
        if 5 in chunks:
            out_toks.append(s.dma("pool", qT[:, :, g * 512:(g + 1) * 512].rearrange("c p t -> p c t"),
                                  qt[:], qds[g % 2], reads=[qt]))

    for t in range(min(3, NT)):
        stage_load(t)
    stage_norm(0)
    if NT > 1:
        stage_norm(1)
    stage_tr(0)
    NG_ = NT // 4
    for ti in range(NT):
        if ti + 3 < NT:
            stage_load(ti + 3)
        if ti + 2 < NT:
            stage_norm(ti + 2)
        if ti + 1 < NT:
            stage_tr(ti + 1)
        stage_tm(ti)
        g, tt = divmod(ti, 4)
        if g >= 1 and tt < 3:
            stage_fm(g - 1, [2 * tt, 2 * tt + 1])
    stage_fm(NG_ - 1, [0, 1, 2, 3, 4, 5])
    tri = s.sb("tri", [128, 128], BF16)
    su = s.sb("su", [128, 128], BF16)
    onesb = s.sb("onesb", [128, 128], BF16)
    sel = s.sb("sel", [128, 128], BF16)
    totrep = s.sb("totrep", [128, 128], F32)
    cpos = s.sb("cpos_sb", [128, 128], F32)
    crefs = s.sb("cref_sb", [128, 128], F32)
    s.op("pool", lambda e: e.memset(onesb[:], 1.0), writes=[onesb])
    s.op("pool", lambda e: e.memset(tri[:], 1.0), writes=[tri])
    s.op("pool", lambda e: e.memset(su[:], 1.0), writes=[su])
    s.op("pool", lambda e: e.memset(sel[:], 1.0), writes=[sel])
    s.op("pool", lambda e: e.affine_select(out=tri[:], in_=tri[:], pattern=[[1, 128]],
                                           compare_op=ALU.is_ge, fill=0.0, base=0, channel_multiplier=-1),
         reads=[tri], writes=[tri])
    s.op("pool", lambda e: e.affine_select(out=su[:], in_=su[:], pattern=[[1, 128]],
                                           compare_op=ALU.is_gt, fill=0.0, base=0, channel_multiplier=-1),
         reads=[su], writes=[su])
    s.op("pool", lambda e: e.affine_select(out=sel[:], in_=sel[:], pattern=[[0, 128]],
                                           compare_op=ALU.is_ge, fill=0.0, base=-127, channel_multiplier=1),
         reads=[sel], writes=[sel])
    sp3 = [s.sb(f"sp3_{i}", [128, 128], BF16) for i in range(3)]
    spr = s.sb("spr", [128, 128], F32)

    def split3(src):
        s.op("dve", lambda e: e.tensor_copy(sp3[0][:], src[:]), reads=[src], writes=[sp3[0]])
        s.op("dve", lambda e: e.tensor_tensor(out=spr[:], in0=src[:], in1=sp3[0][:], op=ALU.subtract),
             reads=[src, sp3[0]], writes=[spr])
        s.op("dve", lambda e: e.tensor_copy(sp3[1][:], spr[:]), reads=[spr], writes=[sp3[1]])
        s.op("dve", lambda e: e.tensor_tensor(out=spr[:], in0=spr[:], in1=sp3[1][:], op=ALU.subtract),
             reads=[spr, sp3[1]], writes=[spr])
        s.op("dve", lambda e: e.tensor_copy(sp3[2][:], spr[:]), reads=[spr], writes=[sp3[2]])

    split3(lfcol)
    s.group("pe", [lambda e, i=i: e.matmul(psM[:, 0:128], sp3[i][:], onesb[:], start=(i == 0), stop=(i == 2))
                   for i in range(3)], reads=sp3 + [onesb], writes=[psM])
    s.op("dve", lambda e: e.tensor_copy(totrep[:], psM[:, 0:128]), reads=[psM], writes=[totrep])
    s.group("pe", [lambda e, i=i: e.matmul(psA[:, 0:128], tri[:], sp3[i][:], start=(i == 0), stop=False)
                   for i in range(3)], reads=sp3 + [tri], writes=[psA])
    split3(totrep)
    s.group("pe", [lambda e, i=i: e.matmul(psA[:, 0:128], sp3[i][:], su[:], start=False, stop=(i == 2))
                   for i in range(3)], reads=sp3 + [su], writes=[psA])
    s.op("dve", lambda e: e.tensor_copy(cpos[:], psA[:, 0:128]), reads=[psA], writes=[cpos])
    split3(cpos)
    s.group("pe", [lambda e, i=i: e.matmul(psB[:, 0:128], sel[:], sp3[i][:], start=(i == 0), stop=(i == 2))
                   for i in range(3)], reads=sp3 + [sel], writes=[psB])
    s.op("dve", lambda e: e.tensor_copy(crefs[:], psB[:, 0:128]), reads=[psB], writes=[crefs])
    out_toks.append(s.dma("sp", cpos_out, cpos[:, 0:NT], s.dsem("o_cpos"), reads=[cpos]))
    out_toks.append(s.dma("sp", cref_out, crefs[:, 0:NT], s.dsem("o_cref"), reads=[crefs]))
    return out_toks


def build_proj(S):
    nc = bass.Bass("TRN2", target_bir_lowering=False)
    NT = S // 128
    x = nc.dram_tensor("x", [S, D], F32, kind="ExternalInput").ap()
    ng = nc.dram_tensor("ng", [1, D], F32, kind="ExternalInput").ap()
    wc = nc.dram_tensor("wc", [D, WC_COLS], F32, kind="ExternalInput").ap()
    qkg = nc.dram_tensor("qkg", [128, 4], F32, kind="ExternalInput").ap()
    bfv = nc.dram_tensor("bfv", [128, 1], F32, kind="ExternalInput").ap()
    qT = nc.dram_tensor("qT", [6, 128, S], BF16, kind="ExternalOutput").ap()
    vg = nc.dram_tensor("vg", [6, 128, NT, 128], BF16, kind="ExternalOutput").ap()
    cpos = nc.dram_tensor("cpos", [128, NT], F32, kind="ExternalOutput").ap()
    cref = nc.dram_tensor("cref", [128, NT], F32, kind="ExternalOutput").ap()
    with ExitStack() as es:
        s = Sch(nc, es)
        C = _consts(s)
        toks = emit_proj(s, S, x, ng, wc, qkg, bfv, qT, vg, cpos, cref, C)
        s.finish(toks)
        print("proj instrs", s.ninstr)
    return nc


def proj_inputs(x_b, norm_g_l, w_in_l, b_f_l, qk_norm_g_l, h):
    cols = []
    for n, part in ((0, 0), (0, 1), (1, 0), (1, 1), (2, 0), (2, 1)):
        base = n * 4 * 512 + part * 512 + h * 128
        cols.append(np.arange(base, base + 128))
    for n in range(3):
        base = n * 4 * 512 + 2 * 512 + h * 128
        cols.append(np.arange(base, base + 128))
    for n in range(3):
        base = n * 4 * 512 + 3 * 512 + h * 128
        cols.append(np.arange(base, base + 128))
    cols.append(np.array([QKVG + h]))
    cols = np.concatenate(cols)
    wc = np.ascontiguousarray(w_in_l[:, cols])
    qkg = np.ascontiguousarray(qk_norm_g_l.T)
    bfv = np.full((128, 1), b_f_l[h], np.float32)
    return {"x": np.ascontiguousarray(x_b), "ng": np.ascontiguousarray(norm_g_l[None, :]), "wc": wc,
            "qkg": qkg, "bfv": bfv}


def attn_consts_host():
    k = np.arange(128)[:, None]
    q = np.arange(512)[None, :]
    mA = np.stack([(q >= k + 128 * j) for j in range(4)]).astype(np.float32)
    mC = np.stack([(q > k + 128 * j) for j in range(4)]).astype(np.float32)
    mB = []
    for j in range(-4, 4):
        kk = 128 * j + k
        cd = (q // 64) - np.floor_divide(kk, 64)
        mB.append(((cd >= 0) & (cd <= 8)).astype(np.float32))
    mB = np.stack(mB)
    bf = ml_dtypes.bfloat16
    return mA.astype(bf), mB.astype(bf), mC.astype(bf)


def relT_host(rel_bias_h):
    k = np.arange(128)[:, None]
    q = np.arange(512)[None, :]
    out = []
    for j in range(-4, 4):
        diff = q - (128 * j + k)
        idx = np.clip(diff, -128, 128) + 128
        out.append(rel_bias_h[idx])
    return np.ascontiguousarray(np.stack(out).astype(np.float32))


import os as _os
WARM_A = int(_os.environ.get("WARM_A", "0"))
WARM_N = int(_os.environ.get("WARM_N", "256"))


def emit_attn(s, S, qT, vg, cpos_d, cref_d, relT_d, mA_d, mB_d, mC_d, ysT, C, mixers=(0, 1, 2)):
    NT = S // 128
    NB = S // 512
    ident = C["ident"]
    kT = s.sb("kT", [128, S], BF16)
    vS = s.sb("vS", [128, NT, 129], BF16)
    cpos = s.sb("a_cpos", [128, NT], F32)
    cref = s.sb("a_cref", [128, NT], F32)
    relT = s.sb("a_relT", [128, 8, 512], F32)
    mA = s.sb("a_mA", [128, 4, 512], BF16)
    mB = s.sb("a_mB", [128, 8, 512], BF16)
    mC = s.sb("a_mC", [128, 4, 512], BF16)
    onesb = s.sb("a_ones", [128, 128], BF16)
    nu = s.sb("a_nu", [128, 128], BF16)
    negones = s.sb("a_negones", [128, 128], BF16)
    qt = [s.sb(f"a_qt{i}", [128, 512], BF16) for i in range(2)]
    qtd = [s.dsem(f"a_qtd{i}") for i in range(2)]
    sgt = [s.sb(f"a_sg{i}", [128, 4, 128], BF16) for i in range(2)]
    sgd = [s.dsem(f"a_sgd{i}") for i in range(2)]
    bias = [s.sb(f"a_bias{i}", [128, NT], F32) for i in range(2)]
    P = [s.sb(f"a_P{i}", [128, 512], BF16) for i in range(3)]
    L = [s.sb(f"a_L{i}", [128, 512], BF16) for i in range(3)]
    Eb = [s.sb(f"a_E{i}", [128, 512], F32) for i in range(2)]
    tmp = [s.sb(f"a_tmp{i}", [128, 512], F32) for i in range(3)]
    carry = s.sb("a_carry", [128, 512], F32)
    rden = s.sb("a_rden", [128, 4], F32)
    ybf = s.sb("a_ybf", [128, 4, 128], BF16)
    yT = [s.sb(f"a_yT{i}", [128, 512], BF16) for i in range(2)]
    yTd = [s.dsem(f"a_yTd{i}") for i in range(2)]
    psS = [s.ps(f"a_psS{i}", [128, 512], F32) for i in range(2)]
    psR = [s.ps(f"a_psR{i}", [128, 512], F32) for i in range(2)]
    psC = s.ps("a_psC", [128, 512], F32)
    psO = s.ps("a_psO", [128, 2, 512], F32)
    psTt = s.ps("a_psT", [128, 1024], BF16)
    psT = Buf(psTt.ap[:, 0:512], "a_psT_v")
    kd = s.dsem("a_kd")
    vd = s.dsem("a_vd")

    s.dma("sp", cpos[:], cpos_d, s.dsem("a_i0"), writes=[cpos])
    s.dma("sp", cref[:], cref_d, s.dsem("a_i1"), writes=[cref])
    s.dma("sp", relT[:], relT_d.rearrange("j k q -> k j q"), s.dsem("a_i2"), writes=[relT])
    s.dma("sp", mA[:], mA_d.rearrange("j k q -> k j q"), s.dsem("a_i3"), writes=[mA])
    s.dma("sp", mB[:], mB_d.rearrange("j k q -> k j q"), s.dsem("a_i4"), writes=[mB])
    s.dma("sp", mC[:], mC_d.rearrange("j k q -> k j q"), s.dsem("a_i5"), writes=[mC])
    s.op("pool", lambda e: e.memset(onesb[:], 1.0), writes=[onesb])
    s.op("pool", lambda e: e.memset(negones[:], -1.0), writes=[negones])
    s.op("pool", lambda e: e.memset(nu[:], -1.0), writes=[nu])
    s.op("pool", lambda e: e.affine_select(out=nu[:], in_=nu[:], pattern=[[-1, 128]],
                                           compare_op=ALU.is_ge, fill=0.0, base=0, channel_multiplier=1),
         reads=[nu], writes=[nu])
    s.op("pool", lambda e: e.memset(vS[:, :, 128:129], 1.0), writes=[vS])
    if 1 in mixers:
        mneg = s.sb("a_mneg", [128, 8, 512], F32)
        s.op("dve", lambda e: e.tensor_scalar(out=mneg[:], in0=mB[:], scalar1=30000.0, scalar2=-30000.0,
                                              op0=ALU.mult, op1=ALU.add), reads=[mB], writes=[mneg])
        s.op("dve", lambda e: e.tensor_tensor(out=relT[:], in0=relT[:], in1=mB[:], op=ALU.mult),
             reads=[relT, mB], writes=[relT])
        s.op("dve", lambda e: e.tensor_tensor(out=relT[:], in0=relT[:], in1=mneg[:], op=ALU.add),
             reads=[relT, mneg], writes=[relT])

    out_toks = []
    nblk = [0]
    nload = [0]

    def load_kv(m):
        kidx = 2 * m + 1
        s.dma("sp", kT[:], qT[kidx, :, :], kd, writes=[kT])
        for t0 in range(0, NT, 32):
            t1 = min(NT, t0 + 32)
            s.dma("sp", vS[:, t0:t1, 0:128], vg[m, :, t0:t1, :], vd, writes=[vS])

    def load_q(m, qb):
        i = nload[0] % 2
        nload[0] += 1
        s.dma("sp", qt[i][:], qT[2 * m, :, qb * 512:(qb + 1) * 512], qtd[i], writes=[qt[i]])
        s.dma("sp", sgt[i][:], vg[3 + m, :, 4 * qb:4 * qb + 4, :], sgd[i], writes=[sgt[i]])
        return qt[i], sgt[i]

    def po(qs, w):
        return psO[:, qs // 2, (qs % 2) * 256:(qs % 2) * 256 + w]

    def epilogue(m, qb, sg, softmax):
        i = nblk[0] % 2
        if softmax:
            for qs in range(4):
                s.op("dve", lambda e, qs=qs: e.reciprocal(out=rden[:, qs:qs + 1], in_=po(qs, 129)[:, 128:129]),
                     reads=[psO], writes=[rden])
            for qs in range(4):
                s.op("dve", lambda e, qs=qs: e.scalar_tensor_tensor(
                    out=ybf[:, qs, :], in0=po(qs, 128), scalar=rden[:, qs:qs + 1], in1=sg[:, qs, :],
                    op0=ALU.mult, op1=ALU.mult), reads=[psO, rden, sg], writes=[ybf])
        else:
            for qs in range(4):
                s.op("dve", lambda e, qs=qs: e.tensor_tensor(out=ybf[:, qs, :], in0=po(qs, 128),
                                                            in1=sg[:, qs, :], op=ALU.mult),
                     reads=[psO, sg], writes=[ybf])
        s.group("pe", [lambda e, qs=qs: e.transpose(psT[:, qs * 128:(qs + 1) * 128], ybf[:, qs, :], ident[:])
                       for qs in range(4)], reads=[ybf, ident], writes=[psT])
        s.op("dve", lambda e: e.tensor_copy(yT[i][:], psT.ap), reads=[psT], writes=[yT[i]])
        out_toks.append(s.dma("sp", ysT[m, :, qb * 512:(qb + 1) * 512], yT[i][:], yTd[i], reads=[yT[i]]))
        nblk[0] += 1

    def softmax_mixer(m):
        ringS = [psS[0], psS[1], psR[0]]
        load_kv(m)
        tiles = []
        for qb in range(NB):
            kts = list(range(0, 4 * qb + 4)) if m == 0 else list(range(max(0, 4 * qb - 4), 4 * qb + 4))
            for n, kt in enumerate(kts):
                tiles.append((qb, kt, kt - 4 * qb, n == 0, n == len(kts) - 1))
        st = {}

        def stage1(i):
            qb, kt, j, first, last = tiles[i]
            if first:
                st["q"], st["sg"] = load_q(m, qb)
                if m == 0:
                    b = bias[qb % 2]
                    nk = 4 * qb + 4
                    s.op("dve", lambda e: e.tensor_scalar(out=b[:, 0:nk], in0=cpos[:, 0:nk],
                                                          scalar1=cref[:, 4 * qb + 1:4 * qb + 2], scalar2=None,
                                                          op0=ALU.subtract), reads=[cpos, cref], writes=[b])
            q = st["q"]
            tiles[i] = tiles[i] + (q, st["sg"])
            s.op("pe", lambda e: e.matmul(ringS[i % 3][:], kT[:, kt * 128:(kt + 1) * 128], q[:],
                                          start=True, stop=True), reads=[kT, q], writes=[ringS[i % 3]])
            if WARM_A:
                s.group("pe", [lambda e: e.matmul(psR[0][:, 0:WARM_N], kT[:, kt * 128:(kt + 1) * 128], q[:, 0:WARM_N],
                                                  start=True, stop=True) for _ in range(WARM_A)],
                        reads=[kT, q], writes=[psR[0]])

        def stage2(i):
            qb, kt, j, first, last, q, sg = tiles[i]
            p = P[i % 3]
            if m == 0:
                b = bias[qb % 2]
                s.op("act", lambda e: e.activation(out=p[:], in_=ringS[i % 3][:], func=AF.Exp,
                                                   bias=b[:, kt:kt + 1]), reads=[ringS[i % 3], b], writes=[p])
                if j >= 0:
                    s.op("pool", lambda e: e.tensor_tensor(out=p[:], in0=p[:], in1=mA[:, j, :], op=ALU.mult),
                         reads=[p, mA], writes=[p])
            else:
                t = tmp[i % 3]
                s.op("dve", lambda e: e.tensor_tensor(out=t[:], in0=ringS[i % 3][:], in1=relT[:, j + 4, :],
                                                      op=ALU.add), reads=[ringS[i % 3], relT], writes=[t])
                s.op("act", lambda e: e.activation(out=p[:], in_=t[:], func=AF.Exp), reads=[t], writes=[p])
            s.group("pe", [lambda e, qs=qs: e.matmul(po(qs, 129), p[:, qs * 128:(qs + 1) * 128], vS[:, kt, :],
                                                     start=(first and qs % 2 == 0), stop=last,
                                                     skip_group_check=True) for qs in range(4)],
                    reads=[p, vS], writes=[psO])
            if last:
                epilogue(m, qb, sg, True)

        n = len(tiles)
        for i in range(min(2, n)):
            stage1(i)
        for i in range(n):
            if i + 2 < n:
                stage1(i + 2)
            stage2(i)

    def sb_mixer():
        m = 2
        load_kv(m)
        tiles = []
        for qb in range(NB):
            kts = list(range(4 * qb + 3, -1, -1))
            for n, kt in enumerate(kts):
                tiles.append((qb, kt, kt - 4 * qb, n == 0, n == len(kts) - 1))
        st = {}

        def stZ(i):
            qb, kt, j, first, last = tiles[i]
            if first:
                st["q"], st["sg"] = load_q(m, qb)
            q = st["q"]
            tiles[i] = tiles[i] + (q, st["sg"])
            fns = [lambda e: e.matmul(psT.ap, ident[:], kT[:, kt * 128:kt * 128 + 512 if kt * 128 + 512 <= S else kt * 128],
                                      start=True, stop=True) for _ in range(0)]
            fns.append(lambda e: e.matmul(psS[i % 2][:], kT[:, kt * 128:(kt + 1) * 128], q[:],
                                          start=True, stop=True))
            s.group("pe", fns, reads=[kT, q], writes=[psS[i % 2]])

        def stEL(i):
            qb, kt, j, first, last, q, sg = tiles[i]
            E = Eb[i % 2]
            l = L[i % 3]
            s.op("act", lambda e: e.activation(out=E[:], in_=psS[i % 2][:], func=AF.Exp),
                 reads=[psS[i % 2]], writes=[E])
            s.op("act", lambda e: e.activation(out=l[:], in_=E[:], func=AF.Ln, bias=1.0), reads=[E], writes=[l])
            if j >= 0:
                s.op("pool", lambda e: e.tensor_tensor(out=l[:], in0=l[:], in1=mC[:, j, :], op=ALU.mult),
                     reads=[l, mC], writes=[l])

        def stR(i):
            qb, kt, j, first, last, q, sg = tiles[i]
            l = L[i % 3]
            r = psR[i % 2]
            s.group("pe", [lambda e: e.matmul(r[:], nu[:], l[:], start=True, stop=False),
                           lambda e: e.matmul(r[:], kT[:, kt * 128:(kt + 1) * 128], q[:], start=False, stop=True)],
                    reads=[nu, l, kT, q], writes=[r])
            t = tmp[i % 3]
            if first:
                s.op("dve", lambda e: e.tensor_copy(t[:], r[:]), reads=[r], writes=[t])
            else:
                s.op("dve", lambda e: e.tensor_tensor(out=t[:], in0=r[:], in1=carry[:], op=ALU.add),
                     reads=[r, carry], writes=[t])
            if not last:
                s.op("pe", lambda e: e.matmul(psC[:], negones[:], l[:], start=True, stop=True),
                     reads=[negones, l], writes=[psC])
                if first:
                    s.op("dve", lambda e: e.tensor_copy(carry[:], psC[:]), reads=[psC], writes=[carry])
                else:
                    s.op("dve", lambda e: e.tensor_tensor(out=carry[:], in0=psC[:], in1=carry[:], op=ALU.add),
                         reads=[psC, carry], writes=[carry])

        def stA(i):
            qb, kt, j, first, last, q, sg = tiles[i]
            p = P[i % 3]
            t = tmp[i % 3]
            s.op("act", lambda e: e.activation(out=p[:], in_=t[:], func=AF.Exp), reads=[t], writes=[p])
            if j >= 0:
                s.op("pool", lambda e: e.tensor_tensor(out=p[:], in0=p[:], in1=mC[:, j, :], op=ALU.mult),
                     reads=[p, mC], writes=[p])
            s.group("pe", [lambda e, qs=qs: e.matmul(po(qs, 128), p[:, qs * 128:(qs + 1) * 128], vS[:, kt, 0:128],
                                                     start=(first and qs % 2 == 0), stop=last,
                                                     skip_group_check=True) for qs in range(4)],
                    reads=[p, vS], writes=[psO])
            if last:
                epilogue(m, qb, sg, False)

        n = len(tiles)
        stZ(0)
        for i in range(n + 2):
            if i + 1 < n:
                stZ(i + 1)
            if i < n:
                stEL(i)
            if 1 <= i <= n:
                stR(i - 1)
            if i >= 2:
                stA(i - 2)

    for m in mixers:
        if m in (0, 1):
            softmax_mixer(m)
        else:
            sb_mixer()
    return out_toks


def build_attn(S, mixers=(0, 1, 2)):
    nc = bass.Bass("TRN2", target_bir_lowering=False)
    NT = S // 128
    qT = nc.dram_tensor("qT", [6, 128, S], BF16, kind="ExternalInput").ap()
    vg = nc.dram_tensor("vg", [6, 128, NT, 128], BF16, kind="ExternalInput").ap()
    cpos = nc.dram_tensor("cpos", [128, NT], F32, kind="ExternalInput").ap()
    cref = nc.dram_tensor("cref", [128, NT], F32, kind="ExternalInput").ap()
    relT = nc.dram_tensor("relT", [8, 128, 512], F32, kind="ExternalInput").ap()
    mA = nc.dram_tensor("mA", [4, 128, 512], BF16, kind="ExternalInput").ap()
    mB = nc.dram_tensor("mB", [8, 128, 512], BF16, kind="ExternalInput").ap()
    mC = nc.dram_tensor("mC", [4, 128, 512], BF16, kind="ExternalInput").ap()
    ysT = nc.dram_tensor("ysT", [3, 128, S], BF16, kind="ExternalOutput").ap()
    with ExitStack() as es:
        s = Sch(nc, es)
        C = _consts(s)
        toks = emit_attn(s, S, qT, vg, cpos, cref, relT, mA, mB, mC, ysT, C, mixers)
        s.finish(toks)
        print("attn instrs", s.ninstr)
    return nc


def emit_merge(s, T, x, ng, wg, bg, wup, wout, ysT, y, C, TB=1024):
    ident = C["ident"]
    NBLK = T // TB
    NTT = TB // 128
    NTG = TB // 512
    gB = s.sb("m_gB", [128, D], F32)
    bgs = s.sb("m_bg", [128, 48], F32)
    xt = [s.sb(f"m_xt{i}", [128, D], F32) for i in range(3)]
    xds = [s.dsem(f"m_xds{i}") for i in range(3)]
    ss3 = [s.sb(f"m_ss3_{i}", [128, 4], F32) for i in range(2)]
    junk = s.sb("m_junk", [128, D], BF16)
    ss = s.sb("m_ss", [128, 4], F32)
    hn = [s.sb(f"m_hn{i}", [128, D], BF16) for i in range(2)]
    hnT = s.sb("m_hnT", [128, NCH, TB], BF16)
    ys = s.sb("m_ys", [128, 12, TB], BF16)
    ysd = s.dsem("m_ysd")
    mT = s.sb("m_mT", [128, NCH, TB], BF16)
    NWB = 4
    wgc = [s.sb(f"m_wgc{i}", [128, NCH, 128], BF16) for i in range(NWB)]
    wgd = [s.dsem(f"m_wgd{i}") for i in range(NWB)]
    wuc = [s.sb(f"m_wuc{i}", [128, 4, 128], BF16) for i in range(NWB)]
    wud = [s.dsem(f"m_wud{i}") for i in range(NWB)]
    woc = [s.sb(f"m_woc{i}", [128, NCH, 512], BF16) for i in range(2)]
    wod = [s.dsem(f"m_wod{i}") for i in range(2)]
    sig = [s.sb(f"m_sig{i}", [128, 512], F32) for i in range(2)]
    prod = [s.sb(f"m_prod{i}", [128, 512], F32) for i in range(2)]
    acc = [s.sb(f"m_acc{i}", [128, 512], F32) for i in range(NTG)]
    xo = [s.sb(f"m_xo{i}", [128, 512], F32) for i in range(3)]
    xod = [s.dsem(f"m_xod{i}") for i in range(3)]
    xsd = [s.dsem(f"m_xsd{i}") for i in range(3)]
    psTa = s.ps("m_psTa", [128, 1024], BF16)
    psTb = s.ps("m_psTb", [128, 1024], BF16)
    pvA = psTa.ap[:].rearrange("p (c t) -> p c t", c=8)
    pvB = psTb.ap[:].rearrange("p (c t) -> p c t", c=8)
    psG = [s.ps(f"m_psG{i}", [128, 512], F32) for i in range(2)]
    psU = [s.ps(f"m_psU{i}", [128, 512], F32) for i in range(2)]
    psY = [s.ps(f"m_psY{i}", [128, 512], F32) for i in range(2)]

    s.dma("sp", gB[:], ng.partition_broadcast(128), s.dsem("m_i0"), writes=[gB])
    s.dma("sp", bgs[:], bg, s.dsem("m_i1"), writes=[bgs])
    out_toks = []
    nw = 0
    nwi = 0
    nwo = 0
    nx = 0
    nxo = 0
    for blk in range(NBLK):
        t0 = blk * TB
        for n in range(3):
            s.dma("sp", ys[:, n * 4:(n + 1) * 4, :], ysT[n, :, :, t0:t0 + TB].rearrange("h p t -> p h t"),
                  ysd, writes=[ys])
        def m1_load(tt):
            s.dma("sp", xt[tt % 3][:], x[t0 + tt * 128:t0 + (tt + 1) * 128, :], xds[tt % 3], writes=[xt[tt % 3]])

        def m1_norm(tt):
            xs = xt[tt % 3]
            sst = ss3[tt % 2]
            s.op("dve", lambda e: e.memset(sst[:, 0:1], 0.0), writes=[sst])
            s.op("act", lambda e: e.activation(out=junk[:], in_=xs[:], func=AF.Square, accum_out=sst[:, 0:1]),
                 reads=[xs], writes=[junk, sst])
            s.op("act", lambda e: e.activation(out=sst[:, 1:2], in_=sst[:, 0:1], func=AF.Ln,
                                               scale=1.0 / D, bias=EPS), reads=[sst], writes=[sst])
            s.op("act", lambda e: e.activation(out=sst[:, 2:3], in_=sst[:, 1:2], func=AF.Exp, scale=-0.5),
                 reads=[sst], writes=[sst])
            h = hn[tt % 2]
            s.op("dve", lambda e: e.scalar_tensor_tensor(out=h[:], in0=xs[:], scalar=sst[:, 2:3], in1=gB[:],
                                                         op0=ALU.mult, op1=ALU.mult),
                 reads=[xs, sst, gB], writes=[h])

        def m1_tr(tt):
            h = hn[tt % 2]
            s.group("pe", [lambda e, c=c: e.transpose(psTa[:, c * 128:(c + 1) * 128],
                                                      h[:, c * 128:(c + 1) * 128], ident[:])
                           for c in range(8)], reads=[h, ident], writes=[psTa])
            s.group("pe", [lambda e, c=c: e.transpose(psTb[:, c * 128:(c + 1) * 128],
                                                      h[:, (8 + c) * 128:(9 + c) * 128], ident[:])
                           for c in range(8)], reads=[h, ident], writes=[psTb])
            s.op("act", lambda e: e.copy(out=hnT[:, 0:8, tt * 128:(tt + 1) * 128], in_=pvA),
                 reads=[psTa], writes=[hnT])
            s.op("dve", lambda e: e.tensor_copy(hnT[:, 8:16, tt * 128:(tt + 1) * 128], pvB),
                 reads=[psTb], writes=[hnT])

        for tt in range(min(3, NTT)):
            m1_load(tt)
        m1_norm(0)
        if NTT > 1:
            m1_norm(1)
        m1_tr(0)
        for tt in range(NTT):
            if tt + 3 < NTT:
                m1_load(tt + 3)
            if tt + 2 < NTT:
                m1_norm(tt + 2)
            if tt + 1 < NTT:
                m1_tr(tt + 1)
        for dc in range(NCH):
            for n in range(3):
                while nwi < min(nw + NWB, NBLK * 48):
                    ci = nwi % 48
                    s.dma("pool", wgc[nwi % NWB][:], wg[ci], wgd[nwi % NWB], writes=[wgc[nwi % NWB]])
                    s.dma("pool", wuc[nwi % NWB][:], wup[ci], wud[nwi % NWB], writes=[wuc[nwi % NWB]])
                    nwi += 1
                wgb = wgc[nw % NWB]
                wub = wuc[nw % NWB]
                nw += 1
                for tg in range(NTG):
                    pg = psG[tg % 2]
                    pu = psU[tg % 2]
                    s.group("pe", [lambda e, c=c: e.matmul(pg[:], wgb[:, c, :], hnT[:, c, tg * 512:(tg + 1) * 512],
                                                           start=(c == 0), stop=(c == NCH - 1))
                                   for c in range(NCH)], reads=[wgb, hnT], writes=[pg])
                    s.group("pe", [lambda e, k=k: e.matmul(pu[:], wub[:, k, :],
                                                           ys[:, n * 4 + k, tg * 512:(tg + 1) * 512],
                                                           start=(k == 0), stop=(k == 3))
                                   for k in range(4)], reads=[wub, ys], writes=[pu])
                    sg = sig[tg % 2]
                    s.op("act", lambda e: e.activation(out=sg[:], in_=pg[:], func=AF.Sigmoid,
                                                       bias=bgs[:, n * 16 + dc:n * 16 + dc + 1]),
                         reads=[pg, bgs], writes=[sg])
                    a = acc[tg]
                    if n == 0:
                        s.op("dve", lambda e: e.tensor_tensor(out=a[:], in0=pu[:], in1=sg[:], op=ALU.mult),
                             reads=[pu, sg], writes=[a])
                    else:
                        pr = prod[tg % 2]
                        s.op("dve", lambda e: e.tensor_tensor(out=pr[:], in0=pu[:], in1=sg[:], op=ALU.mult),
                             reads=[pu, sg], writes=[pr])
                        if n == 1:
                            s.op("dve", lambda e: e.tensor_tensor(out=a[:], in0=a[:], in1=pr[:], op=ALU.add),
                                 reads=[a, pr], writes=[a])
                        else:
                            s.op("dve", lambda e: e.tensor_tensor(out=mT[:, dc, tg * 512:(tg + 1) * 512],
                                                                  in0=a[:], in1=pr[:], op=ALU.add),
                                 reads=[a, pr], writes=[mT])
        for cg in range(4):
            wo = woc[nwo % 2]
            s.dma("pool", wo[:], wout[cg], wod[nwo % 2], writes=[wo])
            nwo += 1
            for tt in range(NTT):
                xb = xo[nxo % 3]
                rows = slice(t0 + tt * 128, t0 + (tt + 1) * 128)
                s.dma("sp", xb[:], x[rows, cg * 512:(cg + 1) * 512], xod[nxo % 3], writes=[xb])
                py = psY[nxo % 2]
                s.group("pe", [lambda e, c=c: e.matmul(py[:], mT[:, c, tt * 128:(tt + 1) * 128], wo[:, c, :],
                                                       start=(c == 0), stop=(c == NCH - 1))
                               for c in range(NCH)], reads=[mT, wo], writes=[py])
                s.op("dve", lambda e: e.tensor_tensor(out=xb[:], in0=py[:], in1=xb[:], op=ALU.add),
                     reads=[py, xb], writes=[xb])
                out_toks.append(s.dma("sp", y[rows, cg * 512:(cg + 1) * 512], xb[:], xsd[nxo % 3], reads=[xb]))
                nxo += 1
    return out_toks


def build_merge(T, TB=1024):
    nc = bass.Bass("TRN2", target_bir_lowering=False)
    x = nc.dram_tensor("x", [T, D], F32, kind="ExternalInput").ap()
    ng = nc.dram_tensor("ng", [1, D], F32, kind="ExternalInput").ap()
    wg = nc.dram_tensor("wg", [48, 128, NCH, 128], F32, kind="ExternalInput").ap()
    bg = nc.dram_tensor("bg", [128, 48], F32, kind="ExternalInput").ap()
    wup = nc.dram_tensor("wup", [48, 128, 4, 128], F32, kind="ExternalInput").ap()
    wout = nc.dram_tensor("wout", [4, 128, NCH, 512], F32, kind="ExternalInput").ap()
    ysT = nc.dram_tensor("ysT", [3, 4, 128, T], BF16, kind="ExternalInput").ap()
    y = nc.dram_tensor("y", [T, D], F32, kind="ExternalOutput").ap()
    with ExitStack() as es:
        s = Sch(nc, es)
        C = _consts(s)
        toks = emit_merge(s, T, x, ng, wg, bg, wup, wout, ysT, y, C, TB)
        s.finish(toks)
        print("merge instrs", s.ninstr)
    return nc


def build_pa(S):
    nc = bass.Bass("TRN2", target_bir_lowering=False)
    NT = S // 128
    x = nc.dram_tensor("x", [S, D], F32, kind="ExternalInput").ap()
    ng = nc.dram_tensor("ng", [1, D], F32, kind="ExternalInput").ap()
    wc = nc.dram_tensor("wc", [D, WC_COLS], F32, kind="ExternalInput").ap()
    qkg = nc.dram_tensor("qkg", [128, 4], F32, kind="ExternalInput").ap()
    bfv = nc.dram_tensor("bfv", [128, 1], F32, kind="ExternalInput").ap()
    relT = nc.dram_tensor("relT", [8, 128, 512], F32, kind="ExternalInput").ap()
    mA = nc.dram_tensor("mA", [4, 128, 512], BF16, kind="ExternalInput").ap()
    mB = nc.dram_tensor("mB", [8, 128, 512], BF16, kind="ExternalInput").ap()
    mC = nc.dram_tensor("mC", [4, 128, 512], BF16, kind="ExternalInput").ap()
    ysT = nc.dram_tensor("ysT", [3, 128, S], BF16, kind="ExternalOutput").ap()
    qT = nc.dram_tensor("qT_i", [6, 128, S], BF16, kind="Internal").ap()
    vg = nc.dram_tensor("vg_i", [6, 128, NT, 128], BF16, kind="Internal").ap()
    cpos = nc.dram_tensor("cpos_i", [128, NT], F32, kind="Internal").ap()
    cref = nc.dram_tensor("cref_i", [128, NT], F32, kind="Internal").ap()
    with ExitStack() as es:
        s = Sch(nc, es)
        C = _consts(s)
        with s.scope():
            toks = emit_proj(s, S, x, ng, wc, qkg, bfv, qT, vg, cpos, cref, C)
            s.barrier(toks)
        with s.scope():
            toks = emit_attn(s, S, qT, vg, cpos, cref, relT, mA, mB, mC, ysT, C)
            s.finish(toks)
        print("pa instrs", s.ninstr)
    return nc


def merge_weights_host(w_in_l, w_up_l, w_out_l):
    wg = w_in_l[:, QKVG + 4:].reshape(NCH, 128, 3, NCH, 128).transpose(3, 2, 1, 0, 4)
    wg = np.ascontiguousarray(wg).reshape(48, 128, NCH, 128)
    wup = w_up_l.reshape(3, 4, 128, NCH, 128).transpose(3, 0, 2, 1, 4)
    wup = np.ascontiguousarray(wup).reshape(48, 128, 4, 128)
    wout = np.ascontiguousarray(w_out_l.reshape(NCH, 128, 4, 512).transpose(2, 1, 0, 3))
    return wg, wup, wout


BATCH = 2
SEQ = 16384
DEPTH = 2
_PROGS = {}


def _prog(name, fn, *a):
    key = (name,) + a
    if key not in _PROGS:
        _PROGS[key] = fn(*a)
    return _PROGS[key]


def kernel(x, norm_g, w_in, b_f, b_gate, qk_norm_g, rel_bias, w_up, w_out):
    x = np.asarray(x, np.float32)
    norm_g = np.asarray(norm_g, np.float32)
    w_in = np.asarray(w_in, np.float32)
    b_f = np.asarray(b_f, np.float32)
    b_gate = np.asarray(b_gate, np.float32)
    qk_norm_g = np.asarray(qk_norm_g, np.float32)
    rel_bias = np.asarray(rel_bias, np.float32)
    w_up = np.asarray(w_up, np.float32)
    w_out = np.asarray(w_out, np.float32)
    cores = list(range(8))
    mA, mB, mC = attn_consts_host()
    TQ = SEQ // 4
    cur = x
    for l in range(DEPTH):
        ncpa = _prog("pa", build_pa, SEQ)
        ims = []
        for c in cores:
            im = proj_inputs(cur[c // 4], norm_g[l], w_in[l], b_f[l], qk_norm_g[l], c % 4)
            im.update({"relT": relT_host(rel_bias[l, c % 4]), "mA": mA, "mB": mB, "mC": mC})
            ims.append(im)
        ra = run_bass_kernel_spmd(ncpa, ims, core_ids=cores).results
        del ims
        ncm = _prog("merge", build_merge, TQ)
        wg, wup_h, wout_h = merge_weights_host(w_in[l], w_up[l], w_out[l])
        bg = np.ascontiguousarray(b_gate[l].reshape(48, 128).T)
        ims = []
        for c in cores:
            b, tq = c // 4, c % 4
            ysT = np.ascontiguousarray(np.stack([ra[b * 4 + h]["ysT"][:, :, tq * TQ:(tq + 1) * TQ]
                                                 for h in range(4)], axis=1))
            ims.append({"x": np.ascontiguousarray(cur[b, tq * TQ:(tq + 1) * TQ]), "ng": norm_g[l][None, :],
                        "wg": wg, "bg": bg, "wup": wup_h, "wout": wout_h, "ysT": ysT})
        rm = run_bass_kernel_spmd(ncm, ims, core_ids=cores).results
        del ims, ra
        nxt = np.empty_like(cur)
        for c in cores:
            b, tq = c // 4, c % 4
            nxt[b, tq * TQ:(tq + 1) * TQ] = rm[c]["y"]
        cur = nxt
    return cur
```
